# Optimizing a Trainium2 kernel written in Bass

```python
import jax, jax.numpy as jnp
from jax import lax
import numpy as np

D_MODEL = 1024
BATCH = 4
SEQ = 4096
DEPTH = 1

MLA_HEADS = 8
QK_NOPE_DIM = 64
QK_ROPE_DIM = 32
QK_HEAD_DIM = QK_NOPE_DIM + QK_ROPE_DIM
V_HEAD_DIM = 64
Q_LORA_RANK = 384
KV_LORA_RANK = 256
ROPE_BASE = 10000.0
Q_BLOCK = 128
HG_HEADS = 4
HG_KEY_DIM = 128
HG_VAL_DIM = 128
HG_WIDTH_K = HG_HEADS * HG_KEY_DIM
HG_WIDTH_V = HG_HEADS * HG_VAL_DIM
HG_CHUNK = 64
N_BRANCH = 2
BRANCH_WIDTH = MLA_HEADS * V_HEAD_DIM
FFN_HIDDEN = ((8 * D_MODEL // 3 + 255) // 256) * 256
PLE_DIM = 256
EPS = 1e-6

COL_SIZES = (Q_LORA_RANK, KV_LORA_RANK, QK_ROPE_DIM,
             HG_WIDTH_K, HG_WIDTH_K, HG_WIDTH_V, HG_WIDTH_V,
             N_BRANCH * D_MODEL)
IN_COLS = sum(COL_SIZES)

kernel_name = "hybrid_mla_hgrn2_gated_block"


def rms_norm(x, gain):
    xf = x.astype(jnp.float32)
    y = xf * lax.rsqrt(jnp.mean(xf * xf, axis=-1, keepdims=True) + EPS)
    return (y * gain.astype(jnp.float32)).astype(x.dtype)


def apply_rope(x, positions):
    r = x.shape[-1]
    half = r // 2
    inv_freq = jnp.exp(-jnp.log(ROPE_BASE) * jnp.arange(half, dtype=jnp.float32) * 2.0 / r)
    ang = positions.astype(jnp.float32)[..., None] * inv_freq
    cos = jnp.cos(ang)[:, :, None, :]
    sin = jnp.sin(ang)[:, :, None, :]
    xf = x.astype(jnp.float32)
    x1, x2 = xf[..., :half], xf[..., half:]
    out = jnp.concatenate([x1 * cos - x2 * sin, x2 * cos + x1 * sin], axis=-1)
    return out.astype(x.dtype)


def causal_block_attention(q, k, v):
    b, s, h, d = q.shape
    nb = s // Q_BLOCK
    qb = q.reshape(b, nb, Q_BLOCK, h, d).transpose(1, 0, 2, 3, 4)
    kpos = jnp.arange(s)
    scale = d ** -0.5

    def one_block(args):
        qi, bi = args
        sc = jnp.einsum('bqhd,bkhd->bhqk', qi, k, preferred_element_type=jnp.float32) * scale
        qpos = bi * Q_BLOCK + jnp.arange(Q_BLOCK)
        mask = kpos[None, :] <= qpos[:, None]
        sc = jnp.where(mask, sc, -jnp.inf)
        pr = jax.nn.softmax(sc, axis=-1).astype(v.dtype)
        return jnp.einsum('bhqk,bkhd->bqhd', pr, v)

    out = lax.map(one_block, (qb, jnp.arange(nb)))
    return out.transpose(1, 0, 2, 3, 4).reshape(b, s, h, v.shape[-1])


def hgrn2_chunked(q, k, v, log_f):
    b, s, h, kd = q.shape
    vd = v.shape[-1]
    nc = s // HG_CHUNK

    def to_chunks(t):
        return t.reshape(b, nc, HG_CHUNK, h, t.shape[-1]).transpose(1, 0, 3, 2, 4)

    qc, kc, vc, gc = to_chunks(q), to_chunks(k), to_chunks(v), to_chunks(log_f)
    causal = jnp.tril(jnp.ones((HG_CHUNK, HG_CHUNK), dtype=bool))[:, :, None]

    def step(state, inp):
        qi, ki, vi, gi = inp
        cum = jnp.cumsum(gi, axis=2)
        o_inter = jnp.einsum('bhck,bhkv->bhcv', qi * jnp.exp(cum), state)
        diff = cum[:, :, :, None, :] - cum[:, :, None, :, :]
        decay = jnp.exp(jnp.where(causal, diff, -jnp.inf))
        att = jnp.einsum('bhtk,bhsk,bhtsk->bhts', qi, ki, decay)
        o_intra = jnp.einsum('bhts,bhsv->bhtv', att, vi)
        last = cum[:, :, -1:, :]
        new_state = (jnp.exp(last[:, :, 0, :])[..., None] * state
                     + jnp.einsum('bhck,bhcv->bhkv', ki * jnp.exp(last - cum), vi))
        return new_state, o_inter + o_intra

    s0 = jnp.zeros((b, h, kd, vd), jnp.float32)
    _, o = lax.scan(step, s0, (qc, kc, vc, gc))
    return o.transpose(1, 0, 3, 2, 4).reshape(b, s, h, vd)


def setup_inputs(seed: int = 0) -> dict:
    key = jax.random.key(seed)
    ks = jax.random.split(key, 24)

    def w(k, shape, fan_in):
        return jax.random.normal(k, shape, jnp.float32) * (fan_in ** -0.5)

    def gain(k, shape):
        return 1.0 + 0.02 * jax.random.normal(k, shape, jnp.float32)

    x = jax.random.normal(ks[0], (BATCH, SEQ, D_MODEL), jnp.float32)
    p = jax.random.normal(ks[1], (DEPTH, BATCH, SEQ, PLE_DIM), jnp.float32)
    offsets = jax.random.randint(ks[2], (BATCH, 1), 0, 1024, dtype=jnp.int32)
    positions = offsets + jnp.arange(SEQ, dtype=jnp.int32)[None, :]
    return {
        "x": x,
        "p": p,
        "positions": positions,
        "mix_norm_g": gain(ks[3], (DEPTH, D_MODEL)),
        "w_in": w(ks[4], (DEPTH, D_MODEL, IN_COLS), D_MODEL),
        "q_a_norm_g": gain(ks[5], (DEPTH, Q_LORA_RANK)),
        "w_uq": w(ks[6], (DEPTH, Q_LORA_RANK, MLA_HEADS * QK_HEAD_DIM), Q_LORA_RANK),
        "kv_a_norm_g": gain(ks[7], (DEPTH, KV_LORA_RANK)),
        "w_ukv": w(ks[8], (DEPTH, KV_LORA_RANK, MLA_HEADS * (QK_NOPE_DIM + V_HEAD_DIM)), KV_LORA_RANK),
        "q_norm_g": gain(ks[9], (DEPTH, QK_HEAD_DIM)),
        "k_norm_g": gain(ks[10], (DEPTH, QK_HEAD_DIM)),
        "hg_lb_logits": 0.5 * jax.random.normal(ks[11], (DEPTH + 1, HG_WIDTH_K), jnp.float32),
        "hg_out_norm_g": gain(ks[12], (DEPTH, HG_VAL_DIM)),
        "w_branch": w(ks[13], (DEPTH, N_BRANCH, BRANCH_WIDTH, D_MODEL), BRANCH_WIDTH),
        "w_out": w(ks[14], (DEPTH, D_MODEL, D_MODEL), D_MODEL),
        "ffn_norm_g": gain(ks[15], (DEPTH, D_MODEL)),
        "w_ffn_gate": w(ks[16], (DEPTH, D_MODEL, FFN_HIDDEN), D_MODEL),
        "w_ffn_up": w(ks[17], (DEPTH, D_MODEL, FFN_HIDDEN), D_MODEL),
        "w_ffn_down": w(ks[18], (DEPTH, FFN_HIDDEN, D_MODEL), FFN_HIDDEN),
        "ple_gate_norm_g": gain(ks[19], (DEPTH, D_MODEL)),
        "w_ple_gate": w(ks[20], (DEPTH, D_MODEL, D_MODEL), D_MODEL),
        "w_ple_proj": w(ks[21], (DEPTH, PLE_DIM, D_MODEL), PLE_DIM),
        "ple_post_norm_g": gain(ks[22], (DEPTH, D_MODEL)),
    }


def reference(x, p, positions, mix_norm_g, w_in, q_a_norm_g, w_uq, kv_a_norm_g, w_ukv,
              q_norm_g, k_norm_g, hg_lb_logits, hg_out_norm_g, w_branch, w_out,
              ffn_norm_g, w_ffn_gate, w_ffn_up, w_ffn_down,
              ple_gate_norm_g, w_ple_gate, w_ple_proj, ple_post_norm_g):
    b, s, _ = x.shape
    split_points = np.cumsum(COL_SIZES)[:-1].tolist()
    lower_bounds = jnp.cumsum(jax.nn.softmax(hg_lb_logits.astype(jnp.float32), axis=0), axis=0)

    for layer in range(DEPTH):
        h = rms_norm(x, mix_norm_g[layer])
        proj = h @ w_in[layer]
        c_q, c_kv, k_rope_raw, hq, hf, hi, hg, br_gates = jnp.split(proj, split_points, axis=-1)

        q = (rms_norm(c_q, q_a_norm_g[layer]) @ w_uq[layer]).reshape(b, s, MLA_HEADS, QK_HEAD_DIM)
        kv = (rms_norm(c_kv, kv_a_norm_g[layer]) @ w_ukv[layer]).reshape(
            b, s, MLA_HEADS, QK_NOPE_DIM + V_HEAD_DIM)
        k_nope, v = kv[..., :QK_NOPE_DIM], kv[..., QK_NOPE_DIM:]
        k_rope = jnp.broadcast_to(k_rope_raw[:, :, None, :], (b, s, MLA_HEADS, QK_ROPE_DIM))
        k = jnp.concatenate([k_nope, k_rope], axis=-1)
        q = rms_norm(q, q_norm_g[layer])
        k = rms_norm(k, k_norm_g[layer])
        q = jnp.concatenate([q[..., :QK_NOPE_DIM], apply_rope(q[..., QK_NOPE_DIM:], positions)], axis=-1)
        k = jnp.concatenate([k[..., :QK_NOPE_DIM], apply_rope(k[..., QK_NOPE_DIM:], positions)], axis=-1)
        attn = causal_block_attention(q, k, v).reshape(b, s, BRANCH_WIDTH)

        lb = lower_bounds[layer]
        f = lb + (1.0 - lb) * jax.nn.sigmoid(hf.astype(jnp.float32))
        log_f = jnp.log(f)
        hk = 1.0 - f
        o = hgrn2_chunked(
            hq.astype(jnp.float32).reshape(b, s, HG_HEADS, HG_KEY_DIM),
            hk.reshape(b, s, HG_HEADS, HG_KEY_DIM),
            hi.astype(jnp.float32).reshape(b, s, HG_HEADS, HG_VAL_DIM),
            log_f.reshape(b, s, HG_HEADS, HG_KEY_DIM))
        o = rms_norm(o, hg_out_norm_g[layer]) * jax.nn.silu(
            hg.astype(jnp.float32).reshape(b, s, HG_HEADS, HG_VAL_DIM))
        rec = o.reshape(b, s, HG_WIDTH_V).astype(x.dtype)

        branches = jnp.stack([attn, rec], axis=2)
        y = jnp.einsum('bsgc,gcd->bsgd', branches, w_branch[layer])
        gates = jax.nn.sigmoid(br_gates.reshape(b, s, N_BRANCH, D_MODEL))
        x = x + jnp.sum(gates * y, axis=2) @ w_out[layer]

        h2 = rms_norm(x, ffn_norm_g[layer])
        x = x + (jax.nn.silu(h2 @ w_ffn_gate[layer]) * (h2 @ w_ffn_up[layer])) @ w_ffn_down[layer]

        e = rms_norm(p[layer] @ w_ple_proj[layer], ple_post_norm_g[layer])
        g = jax.nn.sigmoid(rms_norm(x, ple_gate_norm_g[layer]) @ w_ple_gate[layer])
        x = x + g * e
    return x
```

```python
import numpy as np
from contextlib import ExitStack
import concourse.bass as bass
import concourse.mybir as mybir
from concourse.bass_utils import run_bass_kernel_spmd

F32, BF16, I32 = mybir.dt.float32, mybir.dt.bfloat16, mybir.dt.int32
ALU = mybir.AluOpType
AF = mybir.ActivationFunctionType
AX = mybir.AxisListType

EPS = 1e-6
NT = 32
OWN0 = 16
NOWN = 16
C_CQ, C_CKV, C_KR, C_HQ, C_HF, C_HI, C_HG, C_GT = 0, 384, 640, 672, 1184, 1696, 2208, 2720
NG_COLS = 2720
FFN_H = 2816
TWO_PI = float(2 * np.pi)
CW1 = 6.28125
CW2 = float(2 * np.pi - 6.28125)


class _Op:
    __slots__ = ("eng", "fn", "deps", "signal", "sigval", "chan", "wsz", "odeps", "cost", "lat", "idx", "aset")


class Prog:
    ENGS = ("sync", "act", "dve", "pool", "pe")

    def __init__(self):
        self.ops = {e: [] for e in self.ENGS}
        self.lastw = {}
        self.readers = {}
        self.chan_n = {}
        self.muted = False
        self.seg = []
        self.nops = 0
        self.do_sched = True
        self.prio_mode = "cp"

    @staticmethod
    def _keys(items):
        out = []
        for it in items:
            if it is None:
                continue
            if isinstance(it, (str, tuple)):
                out.append(it)
            elif hasattr(it, "tensor"):
                out.append(it.tensor.name)
            else:
                out.append(it.name)
        return out

    def add(self, eng, fn, reads=(), writes=(), sreads=(), chan=None, cost=0.1, lat=0.0, nowaw=False):
        if self.muted:
            return None
        op = _Op()
        op.eng, op.fn, op.deps, op.signal, op.sigval, op.chan = eng, fn, {}, False, 0, chan
        op.odeps, op.cost, op.lat, op.idx = {}, cost, lat, self.nops
        op.aset = None
        self.nops += 1
        op.wsz = 1 << 30
        for w_ in writes:
            if w_ is not None and hasattr(w_, "shape") and hasattr(w_, "tensor"):
                n_ = 1
                for d_ in list(w_.shape)[1:]:
                    n_ *= int(d_)
                op.wsz = min(op.wsz, n_)
        if chan is not None:
            self.chan_n[chan] = self.chan_n.get(chan, 0) + 1
            op.sigval = 16 * self.chan_n[chan]

        def dep(o, force=False):
            if o is None or o is op:
                return
            if o.eng == eng and o.chan is None and not force:
                op.odeps[id(o)] = o
                return
            op.deps[id(o)] = o

        rk, sk, wk = self._keys(reads), self._keys(sreads), self._keys(writes)
        wk = wk + [k for k in rk if isinstance(k, str) and k.startswith("bank")]
        rk = [k for k in rk if not (isinstance(k, str) and k.startswith("bank"))]
        for k in rk:
            lw = self.lastw.get(k)
            dep(lw, lw is not None and lw.eng == eng and lw.wsz <= 256)
        for k in sk:
            dep(self.lastw.get(k), True)
        for k in wk:
            lw = self.lastw.get(k)
            if nowaw and lw is not None and lw.chan is not None and lw.chan == chan:
                for o_ in lw.deps.values():
                    dep(o_)
            else:
                dep(lw)
            for r in self.readers.get(k, ()):
                dep(r)
        for k in wk:
            self.lastw[k] = op
            self.readers[k] = []
        for k in rk + sk:
            self.readers.setdefault(k, []).append(op)
        self.seg.append(op)
        return op

    def flush_segment(self):
        import heapq
        ops = self.seg
        self.seg = []
        if not ops:
            return
        if not self.do_sched:
            for op in ops:
                self.ops[op.eng].append(op)
            return
        SYNC_ = 1.0
        PRIO = self.prio_mode
        inseg = {id(o) for o in ops}
        preds = {}
        succs = {id(o): [] for o in ops}
        npred = {}
        for o in ops:
            pl = [p for p in list(o.deps.values()) + list(o.odeps.values()) if id(p) in inseg]
            preds[id(o)] = pl
            npred[id(o)] = len(pl)
            for p in pl:
                succs[id(p)].append(o)
        bl = {}
        for o in reversed(ops):
            m_ = o.lat
            for sc in succs[id(o)]:
                v_ = bl[id(sc)] + ((SYNC_ + o.lat) if (sc.eng != o.eng or o.chan is not None) else 0.0)
                if v_ > m_:
                    m_ = v_
            bl[id(o)] = o.cost + m_
        finish = {}
        endt = {}
        free = {e: 0.0 for e in self.ENGS}
        pending = {e: [] for e in self.ENGS}
        ready = {e: [] for e in self.ENGS}
        SYNC = 1.0

        def make_eligible(o):
            dr = 0.0
            for p in preds[id(o)]:
                if p.eng != o.eng or p.chan is not None:
                    t = finish[id(p)] + SYNC
                else:
                    t = endt[id(p)]
                if t > dr:
                    dr = t
            heapq.heappush(pending[o.eng], (dr, o.idx, o))

        for o in ops:
            if npred[id(o)] == 0:
                make_eligible(o)
        out = {e: [] for e in self.ENGS}
        n_done = 0
        cur_set = [None]
        while n_done < len(ops):
            best = None
            for e in self.ENGS:
                pe_, re_ = pending[e], ready[e]
                while pe_ and pe_[0][0] <= free[e]:
                    dr, ix, o = heapq.heappop(pe_)
                    heapq.heappush(re_, ((-bl[id(o)] if PRIO == "cp" else ix), ix, dr, o))
                if re_:
                    st = free[e]
                elif pe_:
                    st = pe_[0][0]
                else:
                    continue
                if best is None or st < best[0]:
                    best = (st, e)
            st, e = best
            if ready[e]:
                if e == "act" and len(ready[e]) > 1:
                    cand = [heapq.heappop(ready[e]) for _ in range(min(len(ready[e]), 6))]
                    pick = 0
                    for ci, (pk_, ix_, dr_, o_) in enumerate(cand):
                        if o_.aset is None or o_.aset == cur_set[0]:
                            pick = ci
                            break
                    pk_, ix, dr, o = cand.pop(pick)
                    for c_ in cand:
                        heapq.heappush(ready[e], c_)
                else:
                    pk_, ix, dr, o = heapq.heappop(ready[e])
            else:
                dr, ix, o = heapq.heappop(pending[e])
            if e == "act" and o.aset is not None:
                if cur_set[0] is not None and cur_set[0] != o.aset:
                    free[e] += 1.3
                cur_set[0] = o.aset
            start = max(free[e], dr)
            free[e] = start + o.cost
            finish[id(o)] = start + o.cost + o.lat
            endt[id(o)] = start + o.cost
            out[e].append(o)
            n_done += 1
            for sc in succs[id(o)]:
                npred[id(sc)] -= 1
                if npred[id(sc)] == 0:
                    make_eligible(sc)
        for e in self.ENGS:
            self.ops[e].extend(out[e])
        self.seg_time = getattr(self, "seg_time", []) + [max(free.values())]

    def barrier(self):
        if self.muted:
            return
        self.flush_segment()
        lasts = {}
        for e in self.ENGS:
            for o in reversed(self.ops[e]):
                if o.fn is not None:
                    lasts[e] = o
                    break
        for e in self.ENGS:
            op = _Op()
            op.eng, op.fn, op.signal, op.sigval, op.chan, op.wsz = e, None, False, 0, None, 1 << 30
            op.aset = None
            op.deps = {id(o): o for ee, o in lasts.items() if ee != e}
            self.ops[e].append(op)

    def finalize(self):
        self.flush_segment()
        for e in self.ENGS:
            for op in self.ops[e]:
                for o in op.deps.values():
                    o.signal = True
        for e in self.ENGS:
            n = 0
            for op in self.ops[e]:
                if op.chan is None and op.signal:
                    n += 1
                    op.sigval = n

    def emit(self, eng_name, eng, sems, chans):
        waited = {}
        for op in self.ops[eng_name]:
            for o in op.deps.values():
                sem = chans[o.chan] if o.chan is not None else sems[o.eng]
                key = id(sem)
                if waited.get(key, 0) >= o.sigval:
                    continue
                eng.wait_ge(sem, o.sigval)
                waited[key] = o.sigval
            if op.fn is None:
                continue
            ins = op.fn(eng)
            if op.chan is not None:
                ins.then_inc(chans[op.chan], 16)
            elif op.signal:
                ins.then_inc(sems[eng_name], 1)


def build_nc(stage="Z", dbg=None, a_tiles=None, substage=9):
    nc = bass.Bass("TRN2", target_bir_lowering=False)
    P = Prog()
    dbg_specs = []

    def din(name, shape, dt=F32):
        return nc.dram_tensor(name, shape, dt, kind="ExternalInput").ap()

    xw = din("xw", [4096, 1024])
    pw = din("pw", [2048, 256])
    posw = din("posw", [128, 32], I32)
    kvalw = din("kvalw", [128, 32])
    w_in = din("w_in", [1024, 4768])
    w_uq = din("w_uq", [384, 768])
    w_ukv = din("w_ukv", [256, 1024])
    w_branch = din("w_branch", [1024, 1024])
    w_out = din("w_out", [1024, 1024])
    w_fg = din("w_fg", [1024, FFN_H])
    w_fu = din("w_fu", [1024, FFN_H])
    w_fd = din("w_fd", [FFN_H, 1024])
    w_pg = din("w_pg", [1024, 1024])
    w_pp = din("w_pp", [256, 1024])
    mixg_d = din("mixg", [128, 8])
    ffng_d = din("ffng", [128, 8])
    plgg_d = din("plgg", [128, 8])
    qag_d = din("qag", [128, 3])
    kvag_d = din("kvag", [128, 2])
    qng_d = din("qng", [1, 96])
    kng_d = din("kng", [1, 96])
    hgg_d = din("hgg", [1, 128])
    ppg_d = din("ppg", [1, 1024])
    lbl_d = din("lbl", [2, 512])
    y = nc.dram_tensor("y", [2048, 1024], F32, kind="ExternalOutput").ap()

    es0 = ExitStack()
    dbg_out = {}

    def dump(name, ap):
        if dbg is None or name not in dbg:
            return
        shp = list(ap.shape)
        d = nc.dram_tensor("dbg_" + name, shp, F32, kind="ExternalOutput").ap()
        dbg_out[name] = d
        P.add("pool", lambda e: e.dma_start(out=d, in_=ap, max_dma_last_dim=2048), reads=[ap], writes=[d, ("yout", 0)],
              chan="dbg_" + name)

    def sb(es, name, shape, dt=F32):
        return es.enter_context(nc.sbuf_tensor(name, shape, dt))

    banks = [es0.enter_context(nc.psum_tensor(f"bank{i}", [128, 512], F32)) for i in range(8)]

    def _n(ap):
        n = 1
        for d_ in list(ap.shape)[1:]:
            n *= int(d_)
        return n

    def dma(q, out, in_, chan, reads=(), writes=(), nowaw=False, **kw):
        nbytes = _n(out) * int(out.shape[0]) * 4
        P.add(q, lambda e: e.dma_start(out=out, in_=in_, **kw), reads=[in_] + list(reads),
              writes=[out] + list(writes), chan=chan, cost=0.15, lat=2.0 + nbytes / 120e3, nowaw=nowaw)

    def act(out, in_, func, bias=0.0, scale=1.0, accum=None, extra_r=(), sr=()):
        srl = list(sr)
        for v in (bias, scale):
            if not isinstance(v, (int, float)):
                srl.append(v)
        kw = {}
        if accum is not None:
            kw["accum_out"] = accum
        op_ = P.add("act", lambda e: e.activation(out=out, in_=in_, func=func, bias=bias, scale=scale, **kw),
                    reads=[in_] + list(extra_r), writes=[out, accum], sreads=srl,
                    cost=0.22 + _n(out) / 1.2e3 + (0.1 if accum is not None else 0.0))
        if op_ is not None:
            op_.aset = {AF.Exp: "le", AF.Ln: "le", AF.Sigmoid: "sg", AF.Silu: "si", AF.Sin: "tr", AF.Sqrt: "sq"}.get(func)

    def tt(eng, out, in0, in1, op):
        P.add(eng, lambda e: e.tensor_tensor(out=out, in0=in0, in1=in1, op=op), reads=[in0, in1], writes=[out],
              cost=(0.12 + _n(out) / 0.96e3) if eng == "dve" else (0.3 + _n(out) / 0.45e3))

    def ts(eng, out, in0, s1, s2, op0, op1=None):
        srl = [v for v in (s1, s2) if v is not None and not isinstance(v, (int, float))]
        if op1 is None:
            P.add(eng, lambda e: e.tensor_scalar(out=out, in0=in0, scalar1=s1, scalar2=None, op0=op0),
                  reads=[in0], writes=[out], sreads=srl, cost=0.12 + _n(out) / 0.96e3)
        else:
            P.add(eng, lambda e: e.tensor_scalar(out=out, in0=in0, scalar1=s1, scalar2=s2, op0=op0, op1=op1),
                  reads=[in0], writes=[out], sreads=srl, cost=0.12 + _n(out) / 0.96e3)

    def stt(out, in0, scalar, in1, op0, op1):
        srl = [scalar] if not isinstance(scalar, (int, float)) else []
        P.add("dve", lambda e: e.scalar_tensor_tensor(out=out, in0=in0, scalar=scalar, in1=in1, op0=op0, op1=op1),
              reads=[in0, in1], writes=[out], sreads=srl, cost=0.12 + _n(out) / 0.96e3)

    def cp(eng, out, in_):
        P.add(eng, lambda e: e.tensor_copy(out=out, in_=in_), reads=[in_], writes=[out],
              cost=(0.12 + _n(out) / 0.96e3) if eng == "dve" else (0.3 + _n(out) / 0.45e3))

    def recip(out, in_):
        P.add("dve", lambda e: e.reciprocal(out=out, in_=in_), reads=[in_], writes=[out], cost=0.12 + _n(out) / 0.96e3)

    def rsum(out, in_):
        P.add("dve", lambda e: e.reduce_sum(out=out, in_=in_, axis=AX.X), reads=[in_], writes=[out],
              cost=0.12 + _n(in_) / 0.96e3)

    def memset(eng, ap, val):
        P.add(eng, lambda e: e.memset(ap, val), writes=[ap], cost=0.1 + _n(ap) / 0.96e3)

    def mm(out, lhsT, rhs, start, stop, extra_r=()):
        f32 = (rhs.dtype == F32)
        P.add("pe", lambda e: e.matmul(out, lhsT=lhsT, rhs=rhs, start=start, stop=stop, skip_group_check=True),
              reads=[lhsT, rhs] + list(extra_r), writes=[out],
              cost=(0.02 + max(_n(rhs), 64) / 2.4e3) * (4 if f32 else 1), lat=0.1)

    def tr(out, in_, ident_ap):
        P.add("pe", lambda e: e.transpose(out=out, in_=in_, identity=ident_ap), reads=[in_, ident_ap], writes=[out],
              cost=0.1, lat=0.1)

    def asel(out, in_, pattern, cmp, fill, base, cm):
        P.add("pool", lambda e: e.affine_select(out=out, in_=in_, pattern=pattern, compare_op=cmp, fill=fill,
                                                base=base, channel_multiplier=cm), reads=[in_], writes=[out])

    def wload(es, name, src, kch, col0, ncols, chan, split=1):
        t = sb(es, name, [128, kch, ncols], BF16)
        v = src.rearrange("(c p) n -> p c n", p=128)
        for c0 in range(kch):
            dma("pool", t[:, c0, :], v[:, c0, col0:col0 + ncols], chan, nowaw=True, max_dma_last_dim=4096)
        return t

    ident = sb(es0, "ident", [128, 128], BF16)
    maskT = sb(es0, "maskT", [128, 128], BF16)
    mblk = sb(es0, "mblk", [128, 128], F32)
    Dmat = sb(es0, "Dmat", [128, 128], F32)
    Sel = sb(es0, "Sel", [128, 4], F32)
    ones_f = sb(es0, "ones_f", [128, 128], F32)
    mixg = sb(es0, "mixg_s", [128, 8])
    ffng = sb(es0, "ffng_s", [128, 8])
    plgg = sb(es0, "plgg_s", [128, 8])
    qag = sb(es0, "qag_s", [128, 3])
    kvag = sb(es0, "kvag_s", [128, 2])
    qng = sb(es0, "qng_s", [128, 96])
    kng = sb(es0, "kng_s", [128, 96])
    hgg = sb(es0, "hgg_s", [128, 128])
    ppg = sb(es0, "ppg_s", [128, 1024])
    omlb = sb(es0, "omlb", [128, 512])
    posi = sb(es0, "posi", [128, 32], I32)
    posf = sb(es0, "posf", [128, 32])
    kval = sb(es0, "kval", [128, 32])
    cs = sb(es0, "cs", [128, 32, 32])
    recT = sb(es0, "recT", [128, 4, 2048], BF16)
    attnT = sb(es0, "attnT", [128, 4, 2048], BF16)
    esS = ExitStack()
    csn = sb(esS, "csn", [128, 32, 32])
    csi = sb(esS, "csi", [128, 32, 32], I32)
    lb0 = sb(esS, "lb0", [128, 512])
    lb1 = sb(esS, "lb1", [128, 512])
    invf = sb(esS, "invf", [128, 32])
    offs = sb(esS, "offs", [128, 32])
    identf = sb(esS, "identf", [128, 128], F32)
    gecol = sb(esS, "gecol", [128, 4], F32)

    for i, (dst, src) in enumerate([(mixg, mixg_d), (ffng, ffng_d), (plgg, plgg_d), (qag, qag_d), (kvag, kvag_d),
                                    (posi, posw), (kval, kvalw)]):
        dma("sync", dst[:], src[:, :], f"su{i}")
    for i, (dst, src) in enumerate([(qng, qng_d), (kng, kng_d), (hgg, hgg_d), (ppg, ppg_d),
                                    (lb0, lbl_d[0:1, :]), (lb1, lbl_d[1:2, :])]):
        dma("sync", dst[:], src.partition_broadcast(128), f"sb{i}")

    memset("pool", ones_f[:], 1.0)
    asel(identf[:], ones_f[:], [[-1, 128]], ALU.is_equal, 0.0, 0, 1)
    cp("pool", ident[:], identf[:])
    asel(mblk[:], ones_f[:], [[1, 128]], ALU.is_ge, 0.0, 0, -1)
    cp("pool", maskT[:], mblk[:])
    memset("pool", mblk[0:64, 64:128], 0.0)
    for i in range(4):
        asel(gecol[:, i:i + 1], ones_f[:, 0:1], [[0, 1]], ALU.is_ge, 0.0, -32 * i, 1)
    tt("pool", Sel[:, 0:1], gecol[:, 1:2], gecol[:, 2:3], ALU.subtract)
    tt("pool", Sel[:, 1:2], gecol[:, 0:1], gecol[:, 1:2], ALU.subtract)
    cp("pool", Sel[:, 2:3], gecol[:, 3:4])
    tt("pool", Sel[:, 3:4], gecol[:, 2:3], gecol[:, 3:4], ALU.subtract)
    tt("pool", Dmat[:, 0:64], mblk[:, 0:64], Sel[:, 1:2].to_broadcast([128, 64]), ALU.subtract)
    tt("pool", Dmat[:, 64:128], mblk[:, 64:128], Sel[:, 3:4].to_broadcast([128, 64]), ALU.subtract)
    tt("dve", omlb[:], lb1[:], lb0[:], ALU.subtract)
    act(omlb[:], omlb[:], AF.Sigmoid)
    for i in range(16):
        f = float(np.exp(-np.log(10000.0) * i * 2.0 / 32))
        memset("pool", invf[:, i:i + 1], f)
        memset("pool", invf[:, 16 + i:17 + i], f)
    memset("pool", offs[:, 0:16], 0.0)
    memset("pool", offs[:, 16:32], float(np.pi / 2))
    cp("dve", posf[:], posi[:])
    for t in range(NT):
        stt(cs[:, t, :], invf[:], posf[:, t:t + 1], offs[:], ALU.mult, ALU.add)
    ts("dve", csn[:], cs[:], 1.0 / TWO_PI, None, ALU.mult)
    cp("dve", csi[:], csn[:])
    cp("dve", csn[:], csi[:])
    stt(cs[:], csn[:], -CW1, cs[:], ALU.mult, ALU.add)
    stt(cs[:], csn[:], -CW2, cs[:], ALU.mult, ALU.add)
    ts("dve", cs[:], cs[:], float(np.pi), float(-np.pi), ALU.min, ALU.max)
    act(cs[:], cs[:], AF.Sin)
    dump("cs", cs[:].rearrange("p a b -> p (a b)"))
    dump("Dmat", Dmat[:])
    dump("Sel", Sel[:])
    dump("omlb", omlb[:])
    dump("mblk", mblk[:])
    if stage == "S":
        P.muted = True
    P.barrier()
    esS.close()

    bank_rr = [0]

    def nb(pool=None):
        if pool is None:
            pool = (0, 1, 2, 3, 4, 5, 6, 7)
        b = banks[pool[bank_rr[0] % len(pool)]]
        bank_rr[0] += 1
        return b

    def bf(bank):
        return bank[:].bitcast(BF16)

    def rstd_of(ss, sq, rs, n, add_ss=None):
        act(sq, ss, AF.Ln, bias=EPS, scale=1.0 / n, sr=[ss])
        act(rs, sq, AF.Exp, scale=-0.5, sr=[sq])

    es1 = ExitStack()
    cqnT = sb(es1, "cqnT", [128, 3, 2048], BF16)
    ckvnT = sb(es1, "ckvnT", [128, 2, 4096], BF16)
    kropeR = sb(es1, "kropeR", [128, 32, 32])
    krss_all = sb(es1, "krss_all", [128, 32])
    kgcol = sb(es1, "kgcol", [96, 1])
    qgcol = sb(es1, "qgcol", [96, 1])
    memset("pool", kgcol[:], 1.0)
    memset("pool", qgcol[:], 1.0)
    dma("sync", kgcol[0:64, 0:1], kng_d[0:1, 0:64].rearrange("o n -> n o"), "su_kg")
    dma("sync", qgcol[0:64, 0:1], qng_d[0:1, 0:64].rearrange("o n -> n o"), "su_qg")
    wuq = wload(es1, "wuq", w_uq, 3, 0, 768, "w_uq")
    wukv = wload(es1, "wukv", w_ukv, 2, 0, 1024, "w_ukv")

    esA = ExitStack()
    win_kv = wload(esA, "win_kv", w_in, 8, C_CKV, 288, "w_in_kv")
    win_f = wload(esA, "win_f", w_in, 8, C_HF, 512, "w_in_f")
    win_i = wload(esA, "win_i", w_in, 8, C_HI, 512, "w_in_i")
    win_q = wload(esA, "win_q", w_in, 8, C_CQ, 384, "w_in_q")
    win_hq = wload(esA, "win_hq", w_in, 8, C_HQ, 512, "w_in_hq")
    win_g = wload(esA, "win_g", w_in, 8, C_HG, 512, "w_in_g")
    xt = [sb(esA, f"xt{i}", [128, 1024]) for i in range(2)]
    junk = sb(esA, "junk", [128, 1024], BF16)
    krope = sb(esA, "krope", [128, 32, 32])
    krg2 = [sb(esA, f"krg{i}", [128, 32]) for i in range(2)]
    kra2 = [sb(esA, f"kra{i}", [128, 16]) for i in range(2)]
    krb2 = [sb(esA, f"krb{i}", [128, 16]) for i in range(2)]
    junkr = sb(esA, "junkr", [128, 32])
    hn = sb(esA, "hn", [128, 1024], BF16)
    hT = [sb(esA, f"hT{i}", [128, 8, 128], BF16) for i in range(2)]
    st_small = [dict(ss=sb(esA, f"ss{i}", [128, 4]), sq=sb(esA, f"sq{i}", [128, 4]), rs=sb(esA, f"rs{i}", [128, 4]))
                for i in range(2)]
    lat_s = [dict(ss=sb(esA, f"lss{i}", [128, 2]), sq=sb(esA, f"lsq{i}", [128, 2]), rs=sb(esA, f"lrs{i}", [128, 2]))
             for i in range(2)]
    latn = sb(esA, "latn", [128, 640], BF16)
    sgn = sb(esA, "sgn", [128, 512])
    kk = sb(esA, "kk", [128, 512])
    gl = sb(esA, "gl", [128, 512])
    etmp = sb(esA, "etmp", [128, 512])
    etmp2 = sb(esA, "etmp2", [128, 512])
    kt_2 = [sb(esA, f"kt{i}", [128, 512], BF16) for i in range(2)]
    qt_2 = [sb(esA, f"qt{i}", [128, 512], BF16) for i in range(2)]
    vt_2 = [sb(esA, f"vt{i}", [128, 512], BF16) for i in range(2)]
    sl_ = sb(esA, "sl", [128, 512])
    slg2 = [sb(esA, f"slg{i}", [128, 512]) for i in range(2)]
    kqT2 = [sb(esA, f"kqT{i}", [128, 8, 128], BF16) for i in range(2)]
    ATm2 = [sb(esA, f"ATm{i}", [128, 4, 128], BF16) for i in range(2)]
    esc = sb(esA, "esc", [128, 4, 4])
    eL = sb(esA, "eL", [128, 4, 2])
    Sst = sb(esA, "Sst", [128, 4, 128])
    Spb = [sb(esA, f"Spb{i}", [128, 4, 128], BF16) for i in range(2)]
    Tst = sb(esA, "Tst", [128, 4, 128])
    Ust = sb(esA, "Ust", [128, 4, 128])
    osq = sb(esA, "osq", [128, 512])
    oss = sb(esA, "oss", [128, 4])
    osr = sb(esA, "osr", [128, 4])
    ors = sb(esA, "ors", [128, 4])
    o1 = sb(esA, "o1", [128, 512])
    rec = sb(esA, "rec", [128, 512], BF16)

    memset("dve", Sst[:], 0.0)

    def norm_tile_to_hT(x_ap, sm, hT_t, gain, junk_t, hn_t):
        act(junk_t[:], x_ap, AF.Square, accum=sm["ss"][:, 0:1])
        if substage < 0.2:
            return
        rstd_of(sm["ss"][:, 0:1], sm["sq"][:, 0:1], sm["rs"][:, 0:1], 1024)
        if substage < 0.3:
            return
        act(hn_t[:], x_ap, AF.Identity, scale=sm["rs"][:, 0:1])
        if substage < 0.4:
            return
        bk = nb()
        bv = bf(bk).rearrange("p (c t) -> p c t", c=8)
        for c in range(8):
            tr(bv[:, c, :], hn_t[:, c * 128:(c + 1) * 128], ident[:])
        if substage < 0.5:
            return
        tt("dve", hT_t[:], bv, gain[:, :].unsqueeze(2).to_broadcast([128, 8, 128]), ALU.mult)

    def proj(hT_t, w, col0, ncols):
        bk = nb()
        for c in range(8):
            mm(bk[:, 0:ncols], hT_t[:, c, :], w[:, c, col0:col0 + ncols], c == 0, c == 7)
        return bk

    for t in (range(NT) if a_tiles is None else a_tiles):
        own = t >= OWN0
        o = t - OWN0
        s = t % 2
        dma("sync", xt[s][:], xw[t * 128:(t + 1) * 128, :], f"x{s}")
        norm_tile_to_hT(xt[s][:], st_small[s], hT[s], mixg, junk, hn)
        if substage < 1:
            continue
        ls = lat_s[s]
        kt_, qt_, vt_, slg, kqT, ATm = kt_2[s], qt_2[s], vt_2[s], slg2[s], kqT2[s], ATm2[s]
        b_kv = proj(hT[s], win_kv, 0, 288)
        act(junk[:, 0:256], b_kv[:, 0:256], AF.Square, accum=ls["ss"][:, 0:1])
        cp("dve", krope[:, t, :], b_kv[:, 256:288])
        act(junkr[:], krope[:, t, :], AF.Square, accum=krss_all[:, t:t + 1])
        krg_, kra_, krb_ = krg2[s], kra2[s], krb2[s]
        tt("pool", krg_[:], krope[:, t, :], kng[:, 64:96], ALU.mult)
        sin_t, cos_t = cs[:, t, 0:16], cs[:, t, 16:32]
        tt("pool", kra_[:], krg_[:, 0:16], cos_t, ALU.mult)
        tt("pool", krb_[:], krg_[:, 16:32], sin_t, ALU.mult)
        tt("pool", kropeR[:, t, 0:16], kra_[:], krb_[:], ALU.subtract)
        tt("pool", kra_[:], krg_[:, 16:32], cos_t, ALU.mult)
        tt("pool", krb_[:], krg_[:, 0:16], sin_t, ALU.mult)
        tt("pool", kropeR[:, t, 16:32], kra_[:], krb_[:], ALU.add)
        if substage < 1.1:
            continue
        if own:
            b_q = proj(hT[s], win_q, 0, 384)
            act(junk[:, 256:640], b_q[:, 0:384], AF.Square, accum=ls["ss"][:, 1:2])
        act(ls["sq"][:, 0:1], ls["ss"][:, 0:1], AF.Ln, bias=EPS, scale=1.0 / 256, sr=[ls["ss"]])
        if own:
            act(ls["sq"][:, 1:2], ls["ss"][:, 1:2], AF.Ln, bias=EPS, scale=1.0 / 384, sr=[ls["ss"]])
        nls = 2 if own else 1
        act(ls["rs"][:, 0:nls], ls["sq"][:, 0:nls], AF.Exp, scale=-0.5, sr=[ls["sq"]])
        if substage < 1.2:
            continue
        act(latn[:, 0:256], b_kv[:, 0:256], AF.Identity, scale=ls["rs"][:, 0:1])
        if own:
            act(latn[:, 256:640], b_q[:, 0:384], AF.Identity, scale=ls["rs"][:, 1:2])
        bk = nb()
        bv = bf(bk).rearrange("p (c t) -> p c t", c=8)
        if substage < 1.3:
            continue
        for c in range(5 if own else 2):
            tr(bv[:, c, :], latn[:, c * 128:(c + 1) * 128], ident[:])
        if substage < 1.4:
            continue
        tt("dve", ckvnT[:, :, t * 128:(t + 1) * 128], bv[:, 0:2, :],
           kvag[:, :].unsqueeze(2).to_broadcast([128, 2, 128]), ALU.mult)
        if own:
            tt("dve", cqnT[:, :, o * 128:(o + 1) * 128], bv[:, 2:5, :],
               qag[:, :].unsqueeze(2).to_broadcast([128, 3, 128]), ALU.mult)
        if substage < 2:
            continue
        b_f = proj(hT[s], win_f, 0, 512)
        if own:
            b_g = proj(hT[s], win_g, 0, 512)
        act(sgn[:], b_f[:], AF.Sigmoid, scale=-1.0)
        if own:
            act(sl_[:], b_g[:], AF.Sigmoid)
            tt("dve", sl_[:], b_g[:], sl_[:], ALU.mult)
            tt("pool", slg[:].rearrange("p (h v) -> p h v", h=4), sl_[:].rearrange("p (h v) -> p h v", h=4),
               hgg[:, :].unsqueeze(1).to_broadcast([128, 4, 128]), ALU.mult)
        tt("dve", kk[:], sgn[:], omlb[:], ALU.mult)
        act(gl[:], kk[:], AF.Ln, bias=1.0, scale=-1.0)
        b_D = nb()
        mm(b_D[:], Dmat[:], gl[:], True, True)
        b_s = nb()
        for h in range(4):
            mm(b_s[:, h * 4:(h + 1) * 4], gl[:, h * 128:(h + 1) * 128], Sel[:], True, True)
        b_i = proj(hT[s], win_i, 0, 512)
        cp("dve", vt_[:], b_i[:])
        act(etmp[:], b_D[:], AF.Exp, scale=-1.0)
        tt("dve", kt_[:], kk[:], etmp[:], ALU.mult)
        act(esc[:].rearrange("p h f -> p (h f)"), b_s[:, 0:16], AF.Exp)
        tt("dve", eL[:], esc[:].rearrange("p h (c two) -> p h c two", two=2)[:, :, :, 0],
           esc[:].rearrange("p h (c two) -> p h c two", two=2)[:, :, :, 1], ALU.mult)
        if own:
            b_hq = proj(hT[s], win_hq, 0, 512)
            act(etmp2[:], b_D[:], AF.Exp)
            tt("dve", qt_[:], b_hq[:], etmp2[:], ALU.mult)
            bk2 = nb()
            bv2 = bf(bk2).rearrange("p (c t) -> p c t", c=8)
            for h in range(4):
                tr(bv2[:, h, :], kt_[:, h * 128:(h + 1) * 128], ident[:])
            for h in range(4):
                tr(bv2[:, 4 + h, :], qt_[:, h * 128:(h + 1) * 128], ident[:])
            cp("dve", kqT[:], bv2)
            b_A = nb()
            for h in range(4):
                mm(b_A[:, h * 128:(h + 1) * 128], kqT[:, h, :], kqT[:, 4 + h, :], True, True)
            tt("dve", ATm[:], b_A[:].rearrange("p (h t) -> p h t", h=4),
               mblk[:, :].unsqueeze(1).to_broadcast([128, 4, 128]), ALU.mult)
            b_O = nb()
            for h in range(4):
                mm(b_O[:, h * 128:(h + 1) * 128], ATm[:, h, :], vt_[:, h * 128:(h + 1) * 128], h == 0, False)
        if substage < 3:
            continue
        b_KV = [nb(), nb()]
        for c in range(2):
            for h in range(4):
                mm(b_KV[c][:, h * 128:(h + 1) * 128], kt_[c * 64:(c + 1) * 64, h * 128:(h + 1) * 128],
                   vt_[c * 64:(c + 1) * 64, h * 128:(h + 1) * 128], True, True)
        for c in range(2):
            eA = esc[:, :, 2 * c:2 * c + 1].to_broadcast([128, 4, 128])
            eB = esc[:, :, 2 * c + 1:2 * c + 2].to_broadcast([128, 4, 128])
            eLc = eL[:, :, c:c + 1].to_broadcast([128, 4, 128])
            if own:
                tt("dve", Spb[c][:], Sst[:], eB, ALU.mult)
                for h in range(4):
                    mm(b_O[c * 64:(c + 1) * 64, h * 128:(h + 1) * 128], kqT[:, 4 + h, c * 64:(c + 1) * 64],
                       Spb[c][:, h, :], False, (c == 1 and h == 3))
            tt("pool", Tst[:], Sst[:], eLc, ALU.mult)
            tt("dve", Ust[:], b_KV[c][:].rearrange("p (h v) -> p h v", h=4), eA, ALU.mult)
            tt("dve", Sst[:], Tst[:], Ust[:], ALU.add)
        if substage < 4:
            continue
        if own:
            act(osq[:], b_O[:], AF.Square)
            rsum(oss[:], osq[:].rearrange("p (h v) -> p h v", h=4))
            rstd_of(oss[:], osr[:], ors[:], 128)
            tt("dve", o1[:].rearrange("p (h v) -> p h v", h=4), b_O[:].rearrange("p (h v) -> p h v", h=4),
               ors[:, :].unsqueeze(2).to_broadcast([128, 4, 128]), ALU.mult)
            tt("dve", rec[:], o1[:], slg[:], ALU.mult)
            bk3 = nb()
            bv3 = bf(bk3).rearrange("p (c t) -> p c t", c=8)
            for h in range(4):
                tr(bv3[:, h, :], rec[:, h * 128:(h + 1) * 128], ident[:])
            cp("dve", recT[:, :, o * 128:(o + 1) * 128], bv3[:, 0:4, :])
    dump("recT", recT[:].rearrange("p a b -> p (a b)"))
    dump("cqnT", cqnT[:].rearrange("p a b -> p (a b)"))
    dump("ckvnT", ckvnT[:].rearrange("p a b -> p (a b)"))
    dump("Sst", Sst[:].rearrange("p a b -> p (a b)"))
    if stage == "A":
        P.muted = True
    P.barrier()
    esA.close()

    esB = ExitStack()
    KT = sb(esB, "KT", [96, 4, 4096], BF16)
    Vt = sb(esB, "Vt", [128, 32, 4, 65], BF16)
    NKB = 3
    kvsq2 = [sb(esB, f"kvsq{i}", [128, 512]) for i in range(NKB)]
    kss2 = [sb(esB, f"kss{i}", [128, 4]) for i in range(NKB)]
    ksq2 = [sb(esB, f"ksq{i}", [128, 4]) for i in range(NKB)]
    krs2 = [sb(esB, f"krs{i}", [128, 4]) for i in range(NKB)]
    kfull2 = [sb(esB, f"kfull{i}", [128, 4, 96], BF16) for i in range(NKB)]
    kr42 = [sb(esB, f"kr4{i}", [128, 4, 32]) for i in range(NKB)]
    ra2 = [sb(esB, f"ra{i}", [128, 4, 16]) for i in range(NKB)]
    rb2 = [sb(esB, f"rb{i}", [128, 4, 16]) for i in range(NKB)]
    QT = [sb(esB, f"QT{i}", [96, 4, 512], BF16) for i in range(2)]
    PT = [sb(esB, f"PT{i}", [128, 512], BF16) for i in range(4)]
    dn = sb(esB, "dn", [128, 512])
    rden = sb(esB, "rden", [64, 512])
    SCALE = float(96 ** -0.5)

    def rope(eng, dst, src, t, nh, ra, rb):
        sin_b = cs[:, t, 0:16].unsqueeze(1).to_broadcast([128, nh, 16])
        cos_b = cs[:, t, 16:32].unsqueeze(1).to_broadcast([128, nh, 16])
        x1, x2 = src[:, :, 0:16], src[:, :, 16:32]
        tt(eng, ra[:, 0:nh, :], x1, cos_b, ALU.mult)
        tt(eng, rb[:, 0:nh, :], x2, sin_b, ALU.mult)
        tt(eng, dst[:, :, 64:80], ra[:, 0:nh, :], rb[:, 0:nh, :], ALU.subtract)
        tt(eng, ra[:, 0:nh, :], x2, cos_b, ALU.mult)
        tt(eng, rb[:, 0:nh, :], x1, sin_b, ALU.mult)
        tt(eng, dst[:, :, 80:96], ra[:, 0:nh, :], rb[:, 0:nh, :], ALU.add)

    for half in range(2):
        for t in range(NT):
            u = t % 3
            bk = nb()
            for c in range(2):
                mm(bk[:], ckvnT[:, c, t * 128:(t + 1) * 128], wukv[:, c, half * 512:(half + 1) * 512], c == 0, c == 1)
            kv3 = bk[:].rearrange("p (h d) -> p h d", h=4)
            sq3 = kvsq2[u][:].rearrange("p (h d) -> p h d", h=4)
            act(sq3[:, :, 0:64], kv3[:, :, 0:64], AF.Square)
            rsum(kss2[u][:, 0:4], sq3[:, :, 0:64])
            ts("dve", kss2[u][:, 0:4], kss2[u][:, 0:4], krss_all[:, t:t + 1], None, ALU.add)
            rstd_of(kss2[u][:, 0:4], ksq2[u][:, 0:4], krs2[u][:, 0:4], 96)
            kf = kfull2[u]
            tt("dve", kf[:, :, 0:64], kv3[:, :, 0:64], krs2[u][:, 0:4].unsqueeze(2).to_broadcast([128, 4, 64]), ALU.mult)
            tt("dve", kf[:, :, 64:96], kropeR[:, t, :].unsqueeze(1).to_broadcast([128, 4, 32]),
               krs2[u][:, 0:4].unsqueeze(2).to_broadcast([128, 4, 32]), ALU.mult)
            bk2 = nb()
            bv2 = bf(bk2).rearrange("p (c t) -> p c t", c=8)
            for h in range(4):
                tr(bv2[0:96, h, :], kf[:, h, :], ident[:])
            act(KT[:, :, t * 128:(t + 1) * 128], bv2[0:96, 0:4, :], AF.Identity, scale=kgcol[:, 0:1])
            act(Vt[:, t, :, 0:64], kv3[:, :, 64:128], AF.Identity, scale=kval[:, t:t + 1])
            cp("pool", Vt[:, t, :, 64:65], kval[:, t:t + 1].unsqueeze(1).to_broadcast([128, 4, 1]))
        for g in range(4):
            QTg = QT[g % 2]
            for i in range(4):
                o = g * 4 + i
                t = OWN0 + o
                u = o % 3
                bk = nb((3, 4))
                for c in range(3):
                    mm(bk[:, 0:384], cqnT[:, c, o * 128:(o + 1) * 128], wuq[:, c, half * 384:(half + 1) * 384],
                       c == 0, c == 2)
                q3 = bk[:, 0:384].rearrange("p (h d) -> p h d", h=4)
                act(kvsq2[u][:, 0:384], bk[:, 0:384], AF.Square)
                rsum(kss2[u][:, 0:4], kvsq2[u][:, 0:384].rearrange("p (h d) -> p h d", h=4))
                rstd_of(kss2[u][:, 0:4], ksq2[u][:, 0:4], krs2[u][:, 0:4], 96)
                kf = kfull2[u]
                tt("dve", kf[:, :, 0:64], q3[:, :, 0:64], krs2[u][:, 0:4].unsqueeze(2).to_broadcast([128, 4, 64]), ALU.mult)
                tt("dve", kr42[u][:], q3[:, :, 64:96], krs2[u][:, 0:4].unsqueeze(2).to_broadcast([128, 4, 32]), ALU.mult)
                tt("pool", kr42[u][:], kr42[u][:], qng[:, 64:96].unsqueeze(1).to_broadcast([128, 4, 32]), ALU.mult)
                rope("pool", kf, kr42[u], t, 4, ra2[u], rb2[u])
                bk2 = nb((3, 4))
                bv2 = bf(bk2).rearrange("p (c t) -> p c t", c=8)
                for h in range(4):
                    tr(bv2[0:96, h, :], kf[:, h, :], ident[:])
                act(QTg[:, :, i * 128:(i + 1) * 128], bv2[0:96, 0:4, :], AF.Identity, scale=qgcol[:, 0:1])
            nkt = OWN0 + 4 * g + 4
            for h in range(4):
                hh = half * 4 + h
                b_o = banks[6 + (h % 2)]
                pend = []
                for it in range(nkt + 2):
                    if it < nkt:
                        kt = it
                        j = kt - (OWN0 + 4 * g)
                        qlo = j * 128 if j > 0 else 0
                        b_sc = banks[it % 3]
                        mm(b_sc[:, qlo:512], KT[:, h, kt * 128:(kt + 1) * 128], QTg[:, h, qlo:512], True, True)
                        pt = PT[it % 4]
                        act(pt[:, qlo:512], b_sc[:, qlo:512], AF.Exp, scale=SCALE)
                        if j >= 0:
                            tt("pool", pt[:, qlo:qlo + 128], pt[:, qlo:qlo + 128], maskT[:], ALU.mult)
                        pend.append((kt, qlo, pt))
                    if it >= 2:
                        kt, qlo, pt = pend[it - 2]
                        mm(b_o[0:65, qlo:512], Vt[:, kt, h, :], pt[:, qlo:512], kt == 0, kt == nkt - 1)
                cp("dve", dn[64:65, :], b_o[64:65, :])
                b_d = banks[5]
                mm(b_d[0:64, :], ones_f[64:65, 0:64], dn[64:65, :], True, True)
                recip(rden[:], b_d[0:64, :])
                po = (hh % 2) * 64
                tt("dve", attnT[po:po + 64, hh // 2, g * 512:(g + 1) * 512], b_o[0:64, :], rden[:], ALU.mult)
    dump("attnT", attnT[:].rearrange("p a b -> p (a b)"))
    dump("KT", KT[:].rearrange("p a b -> p (a b)"))
    if stage == "B":
        P.muted = True
    P.barrier()
    esB.close()
    es1.close()

    esC = ExitStack()
    wg = wload(esC, "wg", w_in, 8, C_GT, 2048, "wg", split=4)
    wb = wload(esC, "wb", w_branch, 8, 0, 1024, "wb", split=2)
    xc = [sb(esC, f"xc{i}", [128, 1024]) for i in range(2)]
    junkc = sb(esC, "junkc", [128, 1024], BF16)
    hnc = sb(esC, "hnc", [128, 1024], BF16)
    hTc = [sb(esC, f"hTc{i}", [128, 8, 128], BF16) for i in range(2)]
    smc = [dict(ss=sb(esC, f"css{i}", [128, 4]), sq=sb(esC, f"csq{i}", [128, 4]), rs=sb(esC, f"crs{i}", [128, 4]))
           for i in range(2)]
    gts = sb(esC, "gts", [128, 2048])
    m1 = sb(esC, "m1", [128, 512])
    m2 = sb(esC, "m2", [128, 512])
    mtok = sb(esC, "mtok", [128, 1024], BF16)

    for o in range(NOWN):
        t = OWN0 + o
        s = o % 2
        dma("sync", xc[s][:], xw[t * 128:(t + 1) * 128, :], f"xc{s}")
        norm_tile_to_hT(xc[s][:], smc[s], hTc[s], mixg, junkc, hnc)
        for gq in range(4):
            bk = proj(hTc[s], wg, gq * 512, 512)
            act(gts[:, gq * 512:(gq + 1) * 512], bk[:], AF.Sigmoid)
        for hf in range(2):
            b_a = nb()
            for j in range(4):
                mm(b_a[:], attnT[:, j, o * 128:(o + 1) * 128], wb[:, j, hf * 512:(hf + 1) * 512], j == 0, j == 3)
            b_r = nb()
            for j in range(4):
                mm(b_r[:], recT[:, j, o * 128:(o + 1) * 128], wb[:, 4 + j, hf * 512:(hf + 1) * 512], j == 0, j == 3)
            tt("dve", m1[:], b_a[:], gts[:, hf * 512:(hf + 1) * 512], ALU.mult)
            tt("dve", m2[:], b_r[:], gts[:, 1024 + hf * 512:1024 + (hf + 1) * 512], ALU.mult)
            tt("pool", mtok[:, hf * 512:(hf + 1) * 512], m1[:], m2[:], ALU.add)
        bk = nb()
        bv = bf(bk).rearrange("p (c t) -> p c t", c=8)
        for c in range(8):
            tr(bv[:, c, :], mtok[:, c * 128:(c + 1) * 128], ident[:])
        cp("dve", attnT[:, :, o * 128:(o + 1) * 128], bv[:, 0:4, :])
        cp("dve", recT[:, :, o * 128:(o + 1) * 128], bv[:, 4:8, :])
    P.barrier()
    esC.close()

    es2 = ExitStack()
    x1 = sb(es2, "x1", [128, 16, 1024])
    h2T = sb(es2, "h2T", [128, 8, 2048], BF16)
    esC2 = ExitStack()
    wo = wload(esC2, "wo", w_out, 8, 0, 1024, "wo", split=2)
    xd = [sb(esC2, f"xd{i}", [128, 1024]) for i in range(2)]
    junkd = sb(esC2, "junkd", [128, 1024], BF16)
    hnd = sb(esC2, "hnd", [128, 1024], BF16)
    smc2 = [dict(ss=sb(esC2, f"dss{i}", [128, 4]), sq=sb(esC2, f"dsq{i}", [128, 4]), rs=sb(esC2, f"drs{i}", [128, 4]))
            for i in range(2)]
    h2s = sb(esC2, "h2s", [128, 8, 128], BF16)
    for o in range(NOWN):
        t = OWN0 + o
        s = o % 2
        dma("sync", xd[s][:], xw[t * 128:(t + 1) * 128, :], f"xd{s}")
        for hf in range(2):
            b_z = nb()
            for c in range(8):
                src = attnT if c < 4 else recT
                mm(b_z[:], src[:, c % 4, o * 128:(o + 1) * 128], wo[:, c, hf * 512:(hf + 1) * 512], c == 0, c == 7)
            tt("dve", x1[:, o, hf * 512:(hf + 1) * 512], b_z[:], xd[s][:, hf * 512:(hf + 1) * 512], ALU.add)
        norm_tile_to_hT(x1[:, o, :], smc2[s], h2s, ffng, junkd, hnd)
        cp("pool", h2T[:, :, o * 128:(o + 1) * 128], h2s[:])
    dump("x1", x1[:].rearrange("p a b -> p (a b)"))
    if stage == "C":
        P.muted = True
    P.barrier()
    esC2.close()

    esD = ExitStack()
    NBUF = 3
    hidT2 = [sb(esD, f"hidT{i}", [128, 2, 512], BF16) for i in range(2)]
    sg = [sb(esD, f"sg{i}", [128, 512]) for i in range(2)]
    wfgb = [sb(esD, f"wfgb{i}", [128, 8, 256], BF16) for i in range(NBUF)]
    wfub = [sb(esD, f"wfub{i}", [128, 8, 256], BF16) for i in range(NBUF)]
    wfdb = [sb(esD, f"wfdb{i}", [128, 2, 1024], BF16) for i in range(NBUF)]
    vfg = w_fg.rearrange("(c p) n -> p c n", p=128)
    vfu = w_fu.rearrange("(c p) n -> p c n", p=128)
    vfd = w_fd.rearrange("(c p) n -> p c n", p=128)
    NGRP = 11

    def ffn_load(gi):
        i = gi % NBUF
        for c in range(8):
            dma("pool", wfgb[i][:, c, :], vfg[:, c, gi * 256:(gi + 1) * 256], f"fg{i}", nowaw=True, max_dma_last_dim=4096)
        for c in range(8):
            dma("pool", wfub[i][:, c, :], vfu[:, c, gi * 256:(gi + 1) * 256], f"fu{i}", nowaw=True, max_dma_last_dim=4096)
        for c0 in range(2):
            dma("pool", wfdb[i][:, c0, :], vfd[:, gi * 2 + c0, :], f"fd{i}", nowaw=True, max_dma_last_dim=4096)

    ffn_load(0)
    ffn_load(1)
    cnt = 0
    for gi in range(NGRP):
        if gi + 2 < NGRP:
            ffn_load(gi + 2)
        i = gi % NBUF
        for tb in range(4):
            hid = hidT2[cnt % 2]
            cnt += 1
            for hc in range(2):
                b_g = nb()
                for c in range(8):
                    mm(b_g[:], wfgb[i][:, c, hc * 128:(hc + 1) * 128], h2T[:, c, tb * 512:(tb + 1) * 512], c == 0, c == 7)
                b_u = nb()
                for c in range(8):
                    mm(b_u[:], wfub[i][:, c, hc * 128:(hc + 1) * 128], h2T[:, c, tb * 512:(tb + 1) * 512], c == 0, c == 7)
                sgi = sg[hc % 2]
                act(sgi[:], b_g[:], AF.Silu)
                tt("dve", hid[:, hc, :], b_u[:], sgi[:], ALU.mult)
            for i4 in range(4):
                o = tb * 4 + i4
                for hf in range(2):
                    b_d = nb()
                    for hc in range(2):
                        mm(b_d[:], hid[:, hc, i4 * 128:(i4 + 1) * 128], wfdb[i][:, hc, hf * 512:(hf + 1) * 512],
                           hc == 0, hc == 1)
                    tt("dve", x1[:, o, hf * 512:(hf + 1) * 512], b_d[:], x1[:, o, hf * 512:(hf + 1) * 512], ALU.add)
    dump("x2", x1[:].rearrange("p a b -> p (a b)"))
    if stage == "D":
        P.muted = True
    P.barrier()
    esD.close()

    esE = ExitStack()
    wpg = wload(esE, "wpg", w_pg, 8, 0, 1024, "wpg", split=2)
    wpp = wload(esE, "wpp", w_pp, 2, 0, 1024, "wpp")
    pt_ = [sb(esE, f"pt{i}", [128, 256]) for i in range(2)]
    pb_2 = [sb(esE, f"pb{i}", [128, 256], BF16) for i in range(2)]
    pT2 = [sb(esE, f"pT{i}", [128, 2, 128], BF16) for i in range(2)]
    junke = sb(esE, "junke", [128, 1024], BF16)
    hne2 = [sb(esE, f"hne{i}", [128, 1024], BF16) for i in range(2)]
    h3T2 = [sb(esE, f"h3T{i}", [128, 8, 128], BF16) for i in range(2)]
    sme = [dict(ss=sb(esE, f"ess{i}", [128, 4]), sq=sb(esE, f"esq{i}", [128, 4]), rs=sb(esE, f"ers{i}", [128, 4]))
           for i in range(2)]
    sme2 = [dict(ss=sb(esE, f"fss{i}", [128, 4]), sq=sb(esE, f"fsq{i}", [128, 4]), rs=sb(esE, f"frs{i}", [128, 4]))
            for i in range(2)]
    sgm2 = [sb(esE, f"sgm{i}", [128, 1024]) for i in range(2)]
    e12 = [sb(esE, f"e1{i}", [128, 1024]) for i in range(2)]
    yo = [sb(esE, f"yo{i}", [128, 1024]) for i in range(2)]
    for o in range(NOWN):
        s = o % 2
        pb_, pT, hne, h3T, sgm, e1 = pb_2[s], pT2[s], hne2[s], h3T2[s], sgm2[s], e12[s]
        dma("sync", pt_[s][:], pw[o * 128:(o + 1) * 128, :], f"p{s}")
        cp("pool", pb_[:], pt_[s][:])
        bk = nb()
        bv = bf(bk).rearrange("p (c t) -> p c t", c=8)
        for c in range(2):
            tr(bv[:, c, :], pb_[:, c * 128:(c + 1) * 128], ident[:])
        cp("dve", pT[:], bv[:, 0:2, :])
        b_e = [nb(), nb()]
        sm = sme[s]
        for hf in range(2):
            for c in range(2):
                mm(b_e[hf][:], pT[:, c, :], wpp[:, c, hf * 512:(hf + 1) * 512], c == 0, c == 1)
            act(junke[:, hf * 512:(hf + 1) * 512], b_e[hf][:], AF.Square, accum=sm["ss"][:, hf:hf + 1])
        ts("dve", sm["ss"][:, 2:3], sm["ss"][:, 0:1], sm["ss"][:, 1:2], None, ALU.add)
        rstd_of(sm["ss"][:, 2:3], sm["sq"][:, 2:3], sm["rs"][:, 2:3], 1024)
        norm_tile_to_hT(x1[:, o, :], sme2[s], h3T, plgg, junke, hne)
        for hf in range(2):
            b_g = nb()
            for c in range(8):
                mm(b_g[:], h3T[:, c, :], wpg[:, c, hf * 512:(hf + 1) * 512], c == 0, c == 7)
            act(sgm[:, hf * 512:(hf + 1) * 512], b_g[:], AF.Sigmoid)
            stt(e1[:, hf * 512:(hf + 1) * 512], b_e[hf][:], sm["rs"][:, 2:3], ppg[:, hf * 512:(hf + 1) * 512],
                ALU.mult, ALU.mult)
        tt("pool", e1[:], e1[:], sgm[:], ALU.mult)
        tt("dve", yo[s][:], e1[:], x1[:, o, :], ALU.add)
        dma("sync", y[o * 128:(o + 1) * 128, :], yo[s][:], f"y{s}", writes=[("yout", s)])
    P.muted = False
    P.add("sync", None, reads=[("yout", 0), ("yout", 1)] + [d for d in dbg_out.values()])

    P.finalize()
    chan_names = sorted(P.chan_n.keys())
    with ExitStack() as ess:
        sems = {e: ess.enter_context(nc.semaphore(f"s_{e}")) for e in ("act", "dve", "pool", "pe")}
        chans = {c: ess.enter_context(nc.semaphore(f"c_{c}")) for c in chan_names}
        block = ess.enter_context(nc.Block())

        @block.sync
        def _(e):
            P.emit("sync", e, sems, chans)

        @block.scalar
        def _(e):
            P.emit("act", e, sems, chans)

        @block.vector
        def _(e):
            P.emit("dve", e, sems, chans)

        @block.gpsimd
        def _(e):
            P.emit("pool", e, sems, chans)

        @block.tensor
        def _(e):
            P.emit("pe", e, sems, chans)
    esE.close()
    es2.close()
    es0.close()
    return nc


_NC_CACHE = {}


def _pc(v, k):
    return np.ascontiguousarray(np.asarray(v, np.float32).reshape(k, 128).T)


def kernel(x, p, positions, mix_norm_g, w_in, q_a_norm_g, w_uq, kv_a_norm_g, w_ukv, q_norm_g, k_norm_g,
           hg_lb_logits, hg_out_norm_g, w_branch, w_out, ffn_norm_g, w_ffn_gate, w_ffn_up, w_ffn_down,
           ple_gate_norm_g, w_ple_gate, w_ple_proj, ple_post_norm_g):
    x = np.asarray(x, np.float32)
    p = np.asarray(p, np.float32)
    positions = np.asarray(positions, np.int32)
    f = lambda a: np.ascontiguousarray(np.asarray(a, np.float32))
    shared = {
        "w_in": f(w_in[0]), "w_uq": f(w_uq[0]), "w_ukv": f(w_ukv[0]),
        "w_branch": f(np.asarray(w_branch)[0].reshape(1024, 1024)), "w_out": f(w_out[0]),
        "w_fg": f(w_ffn_gate[0]), "w_fu": f(w_ffn_up[0]), "w_fd": f(w_ffn_down[0]),
        "w_pg": f(w_ple_gate[0]), "w_pp": f(w_ple_proj[0]),
        "mixg": _pc(mix_norm_g[0], 8), "ffng": _pc(ffn_norm_g[0], 8), "plgg": _pc(ple_gate_norm_g[0], 8),
        "qag": _pc(q_a_norm_g[0], 3), "kvag": _pc(kv_a_norm_g[0], 2),
        "qng": f(q_norm_g[0]).reshape(1, 96), "kng": f(k_norm_g[0]).reshape(1, 96),
        "hgg": f(hg_out_norm_g[0]).reshape(1, 128), "ppg": f(ple_post_norm_g[0]).reshape(1, 1024),
        "lbl": f(hg_lb_logits),
    }
    in_maps = []
    for core in range(8):
        b, j = core // 2, core % 2
        if j == 1:
            xwin = x[b]
            pos = positions[b]
            kv = np.ones(4096, np.float32)
        else:
            xwin = np.concatenate([np.zeros((2048, 1024), np.float32), x[b, :2048]], axis=0)
            pos = np.concatenate([np.zeros(2048, np.int32), positions[b, :2048]])
            kv = np.concatenate([np.zeros(2048, np.float32), np.ones(2048, np.float32)])
        m = dict(shared)
        m["xw"] = np.ascontiguousarray(xwin)
        m["pw"] = np.ascontiguousarray(p[0, b, j * 2048:(j + 1) * 2048])
        m["posw"] = np.ascontiguousarray(pos.reshape(32, 128).T.astype(np.int32))
        m["kvalw"] = np.ascontiguousarray(kv.reshape(32, 128).T)
        in_maps.append(m)
    if _NC_CACHE.get("maps_only"):
        return in_maps
    if "nc" not in _NC_CACHE:
        _NC_CACHE["nc"] = build_nc()
    res = run_bass_kernel_spmd(_NC_CACHE["nc"], in_maps, core_ids=list(range(8)))
    out = np.empty((4, 4096, 1024), np.float32)
    for core in range(8):
        b, j = core // 2, core % 2
        out[b, j * 2048:(j + 1) * 2048] = res.results[core]["y"]
    return out
```

```python
import numpy as np
from contextlib import ExitStack
import concourse.bass as bass
import concourse.mybir as mybir
from concourse.bass_utils import run_bass_kernel_spmd

F32, BF16, I32 = mybir.dt.float32, mybir.dt.bfloat16, mybir.dt.int32
ALU = mybir.AluOpType
AF = mybir.ActivationFunctionType
AX = mybir.AxisListType

EPS = 1e-6
NT = 32
OWN0 = 16
NOWN = 16
C_CQ, C_CKV, C_KR, C_HQ, C_HF, C_HI, C_HG, C_GT = 0, 384, 640, 672, 1184, 1696, 2208, 2720
NG_COLS = 2720
FFN_H = 2816
TWO_PI = float(2 * np.pi)
CW1 = 6.28125
CW2 = float(2 * np.pi - 6.28125)


class _Op:
    __slots__ = ("eng", "fn", "deps", "signal", "sigval", "chan", "wsz", "odeps", "cost", "lat", "idx", "aset")


class Prog:
    ENGS = ("sync", "act", "dve", "pool", "pe")

    def __init__(self):
        self.ops = {e: [] for e in self.ENGS}
        self.lastw = {}
        self.readers = {}
        self.chan_n = {}
        self.muted = False
        self.seg = []
        self.nops = 0
        self.do_sched = True
        self.prio_mode = "cp"

    @staticmethod
    def _keys(items):
        out = []
        for it in items:
            if it is None:
                continue
            if isinstance(it, (str, tuple)):
                out.append(it)
            elif hasattr(it, "tensor"):
                out.append(it.tensor.name)
            else:
                out.append(it.name)
        return out

    def add(self, eng, fn, reads=(), writes=(), sreads=(), chan=None, cost=0.1, lat=0.0, nowaw=False):
        if self.muted:
            return None
        op = _Op()
        op.eng, op.fn, op.deps, op.signal, op.sigval, op.chan = eng, fn, {}, False, 0, chan
        op.odeps, op.cost, op.lat, op.idx = {}, cost, lat, self.nops
        op.aset = None
        self.nops += 1
        op.wsz = 1 << 30
        for w_ in writes:
            if w_ is not None and hasattr(w_, "shape") and hasattr(w_, "tensor"):
                n_ = 1
                for d_ in list(w_.shape)[1:]:
                    n_ *= int(d_)
                op.wsz = min(op.wsz, n_)
        if chan is not None:
            self.chan_n[chan] = self.chan_n.get(chan, 0) + 1
            op.sigval = 16 * self.chan_n[chan]

        def dep(o, force=False):
            if o is None or o is op:
                return
            if o.eng == eng and o.chan is None and not force:
                op.odeps[id(o)] = o
                return
            op.deps[id(o)] = o

        rk, sk, wk = self._keys(reads), self._keys(sreads), self._keys(writes)
        wk = wk + [k for k in rk if isinstance(k, str) and k.startswith("bank")]
        rk = [k for k in rk if not (isinstance(k, str) and k.startswith("bank"))]
        for k in rk:
            lw = self.lastw.get(k)
            dep(lw, lw is not None and lw.eng == eng and lw.wsz <= 256)
        for k in sk:
            dep(self.lastw.get(k), True)
        for k in wk:
            lw = self.lastw.get(k)
            if nowaw and lw is not None and lw.chan is not None and lw.chan == chan:
                for o_ in lw.deps.values():
                    dep(o_)
            else:
                dep(lw)
            for r in self.readers.get(k, ()):
                dep(r)
        for k in wk:
            self.lastw[k] = op
            self.readers[k] = []
        for k in rk + sk:
            self.readers.setdefault(k, []).append(op)
        self.seg.append(op)
        return op

    def flush_segment(self):
        import heapq
        ops = self.seg
        self.seg = []
        if not ops:
            return
        if not self.do_sched:
            for op in ops:
                self.ops[op.eng].append(op)
            return
        SYNC_ = 1.0
        PRIO = self.prio_mode
        inseg = {id(o) for o in ops}
        preds = {}
        succs = {id(o): [] for o in ops}
        npred = {}
        for o in ops:
            pl = [p for p in list(o.deps.values()) + list(o.odeps.values()) if id(p) in inseg]
            preds[id(o)] = pl
            npred[id(o)] = len(pl)
            for p in pl:
                succs[id(p)].append(o)
        bl = {}
        for o in reversed(ops):
            m_ = 0.0
            for sc in succs[id(o)]:
                v_ = bl[id(sc)] + (SYNC_ if sc.eng != o.eng else 0.0)
                if v_ > m_:
                    m_ = v_
            bl[id(o)] = o.cost + o.lat + m_
        finish = {}
        endt = {}
        free = {e: 0.0 for e in self.ENGS}
        pending = {e: [] for e in self.ENGS}
        ready = {e: [] for e in self.ENGS}
        SYNC = 1.0

        def make_eligible(o):
            dr = 0.0
            for p in preds[id(o)]:
                if p.eng != o.eng or p.chan is not None:
                    t = finish[id(p)] + SYNC
                else:
                    t = endt[id(p)]
                if t > dr:
                    dr = t
            heapq.heappush(pending[o.eng], (dr, o.idx, o))

        for o in ops:
            if npred[id(o)] == 0:
                make_eligible(o)
        out = {e: [] for e in self.ENGS}
        n_done = 0
        cur_set = [None]
        while n_done < len(ops):
            best = None
            for e in self.ENGS:
                pe_, re_ = pending[e], ready[e]
                while pe_ and pe_[0][0] <= free[e]:
                    dr, ix, o = heapq.heappop(pe_)
                    heapq.heappush(re_, ((-bl[id(o)] if PRIO == "cp" else ix), ix, dr, o))
                if re_:
                    st = free[e]
                elif pe_:
                    st = pe_[0][0]
                else:
                    continue
                if best is None or st < best[0]:
                    best = (st, e)
            st, e = best
            if ready[e]:
                if e == "act" and len(ready[e]) > 1:
                    cand = [heapq.heappop(ready[e]) for _ in range(min(len(ready[e]), 6))]
                    pick = 0
                    for ci, (pk_, ix_, dr_, o_) in enumerate(cand):
                        if o_.aset is None or o_.aset == cur_set[0]:
                            pick = ci
                            break
                    pk_, ix, dr, o = cand.pop(pick)
                    for c_ in cand:
                        heapq.heappush(ready[e], c_)
                else:
                    pk_, ix, dr, o = heapq.heappop(ready[e])
            else:
                dr, ix, o = heapq.heappop(pending[e])
            if e == "act" and o.aset is not None:
                if cur_set[0] is not None and cur_set[0] != o.aset:
                    free[e] += 1.3
                cur_set[0] = o.aset
            start = max(free[e], dr)
            free[e] = start + o.cost
            finish[id(o)] = start + o.cost + o.lat
            endt[id(o)] = start + o.cost
            out[e].append(o)
            n_done += 1
            for sc in succs[id(o)]:
                npred[id(sc)] -= 1
                if npred[id(sc)] == 0:
                    make_eligible(sc)
        for e in self.ENGS:
            self.ops[e].extend(out[e])
        self.seg_time = getattr(self, "seg_time", []) + [max(free.values())]

    def barrier(self):
        if self.muted:
            return
        self.flush_segment()
        lasts = {}
        for e in self.ENGS:
            for o in reversed(self.ops[e]):
                if o.fn is not None:
                    lasts[e] = o
                    break
        for e in self.ENGS:
            op = _Op()
            op.eng, op.fn, op.signal, op.sigval, op.chan, op.wsz = e, None, False, 0, None, 1 << 30
            op.aset = None
            op.deps = {id(o): o for ee, o in lasts.items() if ee != e}
            self.ops[e].append(op)

    def finalize(self):
        self.flush_segment()
        for e in self.ENGS:
            for op in self.ops[e]:
                for o in op.deps.values():
                    o.signal = True
        for e in self.ENGS:
            n = 0
            for op in self.ops[e]:
                if op.chan is None and op.signal:
                    n += 1
                    op.sigval = n

    def emit(self, eng_name, eng, sems, chans):
        waited = {}
        for op in self.ops[eng_name]:
            for o in op.deps.values():
                sem = chans[o.chan] if o.chan is not None else sems[o.eng]
                key = id(sem)
                if waited.get(key, 0) >= o.sigval:
                    continue
                eng.wait_ge(sem, o.sigval)
                waited[key] = o.sigval
            if op.fn is None:
                continue
            ins = op.fn(eng)
            if op.chan is not None:
                ins.then_inc(chans[op.chan], 16)
            elif op.signal:
                ins.then_inc(sems[eng_name], 1)


def build_nc(stage="Z", dbg=None, a_tiles=None, substage=9):
    nc = bass.Bass("TRN2", target_bir_lowering=False)
    P = Prog()
    dbg_specs = []

    def din(name, shape, dt=F32):
        return nc.dram_tensor(name, shape, dt, kind="ExternalInput").ap()

    xw = din("xw", [4096, 1024])
    pw = din("pw", [2048, 256])
    posw = din("posw", [128, 32], I32)
    kvalw = din("kvalw", [128, 32])
    w_in = din("w_in", [1024, 4768])
    w_uq = din("w_uq", [384, 768])
    w_ukv = din("w_ukv", [256, 1024])
    w_branch = din("w_branch", [1024, 1024])
    w_out = din("w_out", [1024, 1024])
    w_fg = din("w_fg", [1024, FFN_H])
    w_fu = din("w_fu", [1024, FFN_H])
    w_fd = din("w_fd", [FFN_H, 1024])
    w_pg = din("w_pg", [1024, 1024])
    w_pp = din("w_pp", [256, 1024])
    mixg_d = din("mixg", [128, 8])
    ffng_d = din("ffng", [128, 8])
    plgg_d = din("plgg", [128, 8])
    qag_d = din("qag", [128, 3])
    kvag_d = din("kvag", [128, 2])
    qng_d = din("qng", [1, 96])
    kng_d = din("kng", [1, 96])
    hgg_d = din("hgg", [1, 128])
    ppg_d = din("ppg", [1, 1024])
    lbl_d = din("lbl", [2, 512])
    y = nc.dram_tensor("y", [2048, 1024], F32, kind="ExternalOutput").ap()

    es0 = ExitStack()
    dbg_out = {}

    def dump(name, ap):
        if dbg is None or name not in dbg:
            return
        shp = list(ap.shape)
        d = nc.dram_tensor("dbg_" + name, shp, F32, kind="ExternalOutput").ap()
        dbg_out[name] = d
        P.add("pool", lambda e: e.dma_start(out=d, in_=ap, max_dma_last_dim=2048), reads=[ap], writes=[d, ("yout", 0)],
              chan="dbg_" + name)

    def sb(es, name, shape, dt=F32):
        return es.enter_context(nc.sbuf_tensor(name, shape, dt))

    banks = [es0.enter_context(nc.psum_tensor(f"bank{i}", [128, 512], F32)) for i in range(8)]

    def _n(ap):
        n = 1
        for d_ in list(ap.shape)[1:]:
            n *= int(d_)
        return n

    def dma(q, out, in_, chan, reads=(), writes=(), nowaw=False, **kw):
        nbytes = _n(out) * int(out.shape[0]) * 4
        P.add(q, lambda e: e.dma_start(out=out, in_=in_, **kw), reads=[in_] + list(reads),
              writes=[out] + list(writes), chan=chan, cost=0.15, lat=2.0 + nbytes / 120e3, nowaw=nowaw)

    def act(out, in_, func, bias=0.0, scale=1.0, accum=None, extra_r=(), sr=()):
        srl = list(sr)
        for v in (bias, scale):
            if not isinstance(v, (int, float)):
                srl.append(v)
        kw = {}
        if accum is not None:
            kw["accum_out"] = accum
        op_ = P.add("act", lambda e: e.activation(out=out, in_=in_, func=func, bias=bias, scale=scale, **kw),
                    reads=[in_] + list(extra_r), writes=[out, accum], sreads=srl,
                    cost=0.17 + _n(out) / 1.2e3 + (0.1 if accum is not None else 0.0))
        if op_ is not None:
            op_.aset = {AF.Exp: "le", AF.Ln: "le", AF.Sigmoid: "sg", AF.Silu: "si", AF.Sin: "tr", AF.Sqrt: "sq"}.get(func)

    def tt(eng, out, in0, in1, op):
        P.add(eng, lambda e: e.tensor_tensor(out=out, in0=in0, in1=in1, op=op), reads=[in0, in1], writes=[out],
              cost=(0.12 + _n(out) / 0.96e3) if eng == "dve" else (0.3 + _n(out) / 0.5e3))

    def ts(eng, out, in0, s1, s2, op0, op1=None):
        srl = [v for v in (s1, s2) if v is not None and not isinstance(v, (int, float))]
        if op1 is None:
            P.add(eng, lambda e: e.tensor_scalar(out=out, in0=in0, scalar1=s1, scalar2=None, op0=op0),
                  reads=[in0], writes=[out], sreads=srl, cost=0.12 + _n(out) / 0.96e3)
        else:
            P.add(eng, lambda e: e.tensor_scalar(out=out, in0=in0, scalar1=s1, scalar2=s2, op0=op0, op1=op1),
                  reads=[in0], writes=[out], sreads=srl, cost=0.12 + _n(out) / 0.96e3)

    def stt(out, in0, scalar, in1, op0, op1):
        srl = [scalar] if not isinstance(scalar, (int, float)) else []
        P.add("dve", lambda e: e.scalar_tensor_tensor(out=out, in0=in0, scalar=scalar, in1=in1, op0=op0, op1=op1),
              reads=[in0, in1], writes=[out], sreads=srl, cost=0.12 + _n(out) / 0.96e3)

    def cp(eng, out, in_):
        P.add(eng, lambda e: e.tensor_copy(out=out, in_=in_), reads=[in_], writes=[out],
              cost=(0.12 + _n(out) / 0.96e3) if eng == "dve" else (0.3 + _n(out) / 0.5e3))

    def recip(out, in_):
        P.add("dve", lambda e: e.reciprocal(out=out, in_=in_), reads=[in_], writes=[out], cost=0.12 + _n(out) / 0.96e3)

    def rsum(out, in_):
        P.add("dve", lambda e: e.reduce_sum(out=out, in_=in_, axis=AX.X), reads=[in_], writes=[out],
              cost=0.12 + _n(in_) / 0.96e3)

    def memset(eng, ap, val):
        P.add(eng, lambda e: e.memset(ap, val), writes=[ap], cost=0.1 + _n(ap) / 0.96e3)

    def mm(out, lhsT, rhs, start, stop, extra_r=()):
        f32 = (rhs.dtype == F32)
        P.add("pe", lambda e: e.matmul(out, lhsT=lhsT, rhs=rhs, start=start, stop=stop, skip_group_check=True),
              reads=[lhsT, rhs] + list(extra_r), writes=[out],
              cost=(0.02 + max(_n(rhs), 64) / 2.4e3) * (4 if f32 else 1), lat=0.1)

    def tr(out, in_, ident_ap):
        P.add("pe", lambda e: e.transpose(out=out, in_=in_, identity=ident_ap), reads=[in_, ident_ap], writes=[out],
              cost=0.1, lat=0.1)

    def asel(out, in_, pattern, cmp, fill, base, cm):
        P.add("pool", lambda e: e.affine_select(out=out, in_=in_, pattern=pattern, compare_op=cmp, fill=fill,
                                                base=base, channel_multiplier=cm), reads=[in_], writes=[out])

    def wload(es, name, src, kch, col0, ncols, chan, split=1):
        t = sb(es, name, [128, kch, ncols], BF16)
        v = src.rearrange("(c p) n -> p c n", p=128)
        for c0 in range(kch):
            dma("pool", t[:, c0, :], v[:, c0, col0:col0 + ncols], chan, nowaw=True, max_dma_last_dim=4096)
        return t

    ident = sb(es0, "ident", [128, 128], BF16)
    maskT = sb(es0, "maskT", [128, 128], BF16)
    mblk = sb(es0, "mblk", [128, 128], F32)
    Dmat = sb(es0, "Dmat", [128, 128], F32)
    Sel = sb(es0, "Sel", [128, 4], F32)
    ones_f = sb(es0, "ones_f", [128, 128], F32)
    mixg = sb(es0, "mixg_s", [128, 8])
    ffng = sb(es0, "ffng_s", [128, 8])
    plgg = sb(es0, "plgg_s", [128, 8])
    qag = sb(es0, "qag_s", [128, 3])
    kvag = sb(es0, "kvag_s", [128, 2])
    qng = sb(es0, "qng_s", [128, 96])
    kng = sb(es0, "kng_s", [128, 96])
    hgg = sb(es0, "hgg_s", [128, 128])
    ppg = sb(es0, "ppg_s", [128, 1024])
    omlb = sb(es0, "omlb", [128, 512])
    posi = sb(es0, "posi", [128, 32], I32)
    posf = sb(es0, "posf", [128, 32])
    kval = sb(es0, "kval", [128, 32])
    cs = sb(es0, "cs", [128, 32, 32])
    recT = sb(es0, "recT", [128, 4, 2048], BF16)
    attnT = sb(es0, "attnT", [128, 4, 2048], BF16)
    esS = ExitStack()
    csn = sb(esS, "csn", [128, 32, 32])
    csi = sb(esS, "csi", [128, 32, 32], I32)
    lb0 = sb(esS, "lb0", [128, 512])
    lb1 = sb(esS, "lb1", [128, 512])
    invf = sb(esS, "invf", [128, 32])
    offs = sb(esS, "offs", [128, 32])
    identf = sb(esS, "identf", [128, 128], F32)
    gecol = sb(esS, "gecol", [128, 4], F32)

    for i, (dst, src) in enumerate([(mixg, mixg_d), (ffng, ffng_d), (plgg, plgg_d), (qag, qag_d), (kvag, kvag_d),
                                    (posi, posw), (kval, kvalw)]):
        dma("sync", dst[:], src[:, :], f"su{i}")
    for i, (dst, src) in enumerate([(qng, qng_d), (kng, kng_d), (hgg, hgg_d), (ppg, ppg_d),
                                    (lb0, lbl_d[0:1, :]), (lb1, lbl_d[1:2, :])]):
        dma("sync", dst[:], src.partition_broadcast(128), f"sb{i}")

    memset("pool", ones_f[:], 1.0)
    asel(identf[:], ones_f[:], [[-1, 128]], ALU.is_equal, 0.0, 0, 1)
    cp("pool", ident[:], identf[:])
    asel(mblk[:], ones_f[:], [[1, 128]], ALU.is_ge, 0.0, 0, -1)
    cp("pool", maskT[:], mblk[:])
    memset("pool", mblk[0:64, 64:128], 0.0)
    for i in range(4):
        asel(gecol[:, i:i + 1], ones_f[:, 0:1], [[0, 1]], ALU.is_ge, 0.0, -32 * i, 1)
    tt("pool", Sel[:, 0:1], gecol[:, 1:2], gecol[:, 2:3], ALU.subtract)
    tt("pool", Sel[:, 1:2], gecol[:, 0:1], gecol[:, 1:2], ALU.subtract)
    cp("pool", Sel[:, 2:3], gecol[:, 3:4])
    tt("pool", Sel[:, 3:4], gecol[:, 2:3], gecol[:, 3:4], ALU.subtract)
    tt("pool", Dmat[:, 0:64], mblk[:, 0:64], Sel[:, 1:2].to_broadcast([128, 64]), ALU.subtract)
    tt("pool", Dmat[:, 64:128], mblk[:, 64:128], Sel[:, 3:4].to_broadcast([128, 64]), ALU.subtract)
    tt("dve", omlb[:], lb1[:], lb0[:], ALU.subtract)
    act(omlb[:], omlb[:], AF.Sigmoid)
    for i in range(16):
        f = float(np.exp(-np.log(10000.0) * i * 2.0 / 32))
        memset("pool", invf[:, i:i + 1], f)
        memset("pool", invf[:, 16 + i:17 + i], f)
    memset("pool", offs[:, 0:16], 0.0)
    memset("pool", offs[:, 16:32], float(np.pi / 2))
    cp("dve", posf[:], posi[:])
    for t in range(NT):
        stt(cs[:, t, :], invf[:], posf[:, t:t + 1], offs[:], ALU.mult, ALU.add)
    ts("dve", csn[:], cs[:], 1.0 / TWO_PI, None, ALU.mult)
    cp("dve", csi[:], csn[:])
    cp("dve", csn[:], csi[:])
    stt(cs[:], csn[:], -CW1, cs[:], ALU.mult, ALU.add)
    stt(cs[:], csn[:], -CW2, cs[:], ALU.mult, ALU.add)
    ts("dve", cs[:], cs[:], float(np.pi), float(-np.pi), ALU.min, ALU.max)
    act(cs[:], cs[:], AF.Sin)
    dump("cs", cs[:].rearrange("p a b -> p (a b)"))
    dump("Dmat", Dmat[:])
    dump("Sel", Sel[:])
    dump("omlb", omlb[:])
    dump("mblk", mblk[:])
    if stage == "S":
        P.muted = True
    P.barrier()
    esS.close()

    bank_rr = [0]

    def nb(pool=None):
        if pool is None:
            pool = (0, 1, 2, 3, 4, 5, 6, 7)
        b = banks[pool[bank_rr[0] % len(pool)]]
        bank_rr[0] += 1
        return b

    def bf(bank):
        return bank[:].bitcast(BF16)

    def rstd_of(ss, sq, rs, n, add_ss=None):
        act(sq, ss, AF.Ln, bias=EPS, scale=1.0 / n, sr=[ss])
        act(rs, sq, AF.Exp, scale=-0.5, sr=[sq])

    es1 = ExitStack()
    cqnT = sb(es1, "cqnT", [128, 3, 2048], BF16)
    ckvnT = sb(es1, "ckvnT", [128, 2, 4096], BF16)
    kropeR = sb(es1, "kropeR", [128, 32, 32])
    krss_all = sb(es1, "krss_all", [128, 32])
    kgcol = sb(es1, "kgcol", [96, 1])
    qgcol = sb(es1, "qgcol", [96, 1])
    memset("pool", kgcol[:], 1.0)
    memset("pool", qgcol[:], 1.0)
    dma("sync", kgcol[0:64, 0:1], kng_d[0:1, 0:64].rearrange("o n -> n o"), "su_kg")
    dma("sync", qgcol[0:64, 0:1], qng_d[0:1, 0:64].rearrange("o n -> n o"), "su_qg")
    wuq = wload(es1, "wuq", w_uq, 3, 0, 768, "w_uq")
    wukv = wload(es1, "wukv", w_ukv, 2, 0, 1024, "w_ukv")

    esA = ExitStack()
    win_kv = wload(esA, "win_kv", w_in, 8, C_CKV, 288, "w_in_kv")
    win_f = wload(esA, "win_f", w_in, 8, C_HF, 512, "w_in_f")
    win_i = wload(esA, "win_i", w_in, 8, C_HI, 512, "w_in_i")
    win_q = wload(esA, "win_q", w_in, 8, C_CQ, 384, "w_in_q")
    win_hq = wload(esA, "win_hq", w_in, 8, C_HQ, 512, "w_in_hq")
    win_g = wload(esA, "win_g", w_in, 8, C_HG, 512, "w_in_g")
    xt = [sb(esA, f"xt{i}", [128, 1024]) for i in range(2)]
    junk = sb(esA, "junk", [128, 1024], BF16)
    krope = sb(esA, "krope", [128, 32, 32])
    krg2 = [sb(esA, f"krg{i}", [128, 32]) for i in range(2)]
    kra2 = [sb(esA, f"kra{i}", [128, 16]) for i in range(2)]
    krb2 = [sb(esA, f"krb{i}", [128, 16]) for i in range(2)]
    junkr = sb(esA, "junkr", [128, 32])
    hn = sb(esA, "hn", [128, 1024], BF16)
    hT = [sb(esA, f"hT{i}", [128, 8, 128], BF16) for i in range(2)]
    st_small = [dict(ss=sb(esA, f"ss{i}", [128, 4]), sq=sb(esA, f"sq{i}", [128, 4]), rs=sb(esA, f"rs{i}", [128, 4]))
                for i in range(2)]
    lat_s = [dict(ss=sb(esA, f"lss{i}", [128, 2]), sq=sb(esA, f"lsq{i}", [128, 2]), rs=sb(esA, f"lrs{i}", [128, 2]))
             for i in range(2)]
    latn = sb(esA, "latn", [128, 640], BF16)
    sgn = sb(esA, "sgn", [128, 512])
    kk = sb(esA, "kk", [128, 512])
    gl = sb(esA, "gl", [128, 512])
    etmp = sb(esA, "etmp", [128, 512])
    etmp2 = sb(esA, "etmp2", [128, 512])
    kt_2 = [sb(esA, f"kt{i}", [128, 512], BF16) for i in range(2)]
    qt_2 = [sb(esA, f"qt{i}", [128, 512], BF16) for i in range(2)]
    vt_2 = [sb(esA, f"vt{i}", [128, 512], BF16) for i in range(2)]
    sl_ = sb(esA, "sl", [128, 512])
    slg2 = [sb(esA, f"slg{i}", [128, 512]) for i in range(2)]
    kqT2 = [sb(esA, f"kqT{i}", [128, 8, 128], BF16) for i in range(2)]
    ATm2 = [sb(esA, f"ATm{i}", [128, 4, 128], BF16) for i in range(2)]
    esc = sb(esA, "esc", [128, 4, 4])
    eL = sb(esA, "eL", [128, 4, 2])
    Sst = sb(esA, "Sst", [128, 4, 128])
    Spb = [sb(esA, f"Spb{i}", [128, 4, 128], BF16) for i in range(2)]
    Tst = sb(esA, "Tst", [128, 4, 128])
    Ust = sb(esA, "Ust", [128, 4, 128])
    osq = sb(esA, "osq", [128, 512])
    oss = sb(esA, "oss", [128, 4])
    osr = sb(esA, "osr", [128, 4])
    ors = sb(esA, "ors", [128, 4])
    o1 = sb(esA, "o1", [128, 512])
    rec = sb(esA, "rec", [128, 512], BF16)

    memset("dve", Sst[:], 0.0)

    def norm_tile_to_hT(x_ap, sm, hT_t, gain, junk_t, hn_t):
        act(junk_t[:], x_ap, AF.Square, accum=sm["ss"][:, 0:1])
        if substage < 0.2:
            return
        rstd_of(sm["ss"][:, 0:1], sm["sq"][:, 0:1], sm["rs"][:, 0:1], 1024)
        if substage < 0.3:
            return
        act(hn_t[:], x_ap, AF.Identity, scale=sm["rs"][:, 0:1])
        if substage < 0.4:
            return
        bk = nb()
        bv = bf(bk).rearrange("p (c t) -> p c t", c=8)
        for c in range(8):
            tr(bv[:, c, :], hn_t[:, c * 128:(c + 1) * 128], ident[:])
        if substage < 0.5:
            return
        tt("dve", hT_t[:], bv, gain[:, :].unsqueeze(2).to_broadcast([128, 8, 128]), ALU.mult)

    def proj(hT_t, w, col0, ncols):
        bk = nb()
        for c in range(8):
            mm(bk[:, 0:ncols], hT_t[:, c, :], w[:, c, col0:col0 + ncols], c == 0, c == 7)
        return bk

    for t in (range(NT) if a_tiles is None else a_tiles):
        own = t >= OWN0
        o = t - OWN0
        s = t % 2
        dma("sync", xt[s][:], xw[t * 128:(t + 1) * 128, :], f"x{s}")
        norm_tile_to_hT(xt[s][:], st_small[s], hT[s], mixg, junk, hn)
        if substage < 1:
            continue
        ls = lat_s[s]
        kt_, qt_, vt_, slg, kqT, ATm = kt_2[s], qt_2[s], vt_2[s], slg2[s], kqT2[s], ATm2[s]
        b_kv = proj(hT[s], win_kv, 0, 288)
        act(junk[:, 0:256], b_kv[:, 0:256], AF.Square, accum=ls["ss"][:, 0:1])
        cp("dve", krope[:, t, :], b_kv[:, 256:288])
        act(junkr[:], krope[:, t, :], AF.Square, accum=krss_all[:, t:t + 1])
        krg_, kra_, krb_ = krg2[s], kra2[s], krb2[s]
        tt("pool", krg_[:], krope[:, t, :], kng[:, 64:96], ALU.mult)
        sin_t, cos_t = cs[:, t, 0:16], cs[:, t, 16:32]
        tt("pool", kra_[:], krg_[:, 0:16], cos_t, ALU.mult)
        tt("pool", krb_[:], krg_[:, 16:32], sin_t, ALU.mult)
        tt("pool", kropeR[:, t, 0:16], kra_[:], krb_[:], ALU.subtract)
        tt("pool", kra_[:], krg_[:, 16:32], cos_t, ALU.mult)
        tt("pool", krb_[:], krg_[:, 0:16], sin_t, ALU.mult)
        tt("pool", kropeR[:, t, 16:32], kra_[:], krb_[:], ALU.add)
        if substage < 1.1:
            continue
        if own:
            b_q = proj(hT[s], win_q, 0, 384)
            act(junk[:, 256:640], b_q[:, 0:384], AF.Square, accum=ls["ss"][:, 1:2])
        act(ls["sq"][:, 0:1], ls["ss"][:, 0:1], AF.Ln, bias=EPS, scale=1.0 / 256, sr=[ls["ss"]])
        if own:
            act(ls["sq"][:, 1:2], ls["ss"][:, 1:2], AF.Ln, bias=EPS, scale=1.0 / 384, sr=[ls["ss"]])
        nls = 2 if own else 1
        act(ls["rs"][:, 0:nls], ls["sq"][:, 0:nls], AF.Exp, scale=-0.5, sr=[ls["sq"]])
        if substage < 1.2:
            continue
        act(latn[:, 0:256], b_kv[:, 0:256], AF.Identity, scale=ls["rs"][:, 0:1])
        if own:
            act(latn[:, 256:640], b_q[:, 0:384], AF.Identity, scale=ls["rs"][:, 1:2])
        bk = nb()
        bv = bf(bk).rearrange("p (c t) -> p c t", c=8)
        if substage < 1.3:
            continue
        for c in range(5 if own else 2):
            tr(bv[:, c, :], latn[:, c * 128:(c + 1) * 128], ident[:])
        if substage < 1.4:
            continue
        tt("dve", ckvnT[:, :, t * 128:(t + 1) * 128], bv[:, 0:2, :],
           kvag[:, :].unsqueeze(2).to_broadcast([128, 2, 128]), ALU.mult)
        if own:
            tt("dve", cqnT[:, :, o * 128:(o + 1) * 128], bv[:, 2:5, :],
               qag[:, :].unsqueeze(2).to_broadcast([128, 3, 128]), ALU.mult)
        if substage < 2:
            continue
        b_f = proj(hT[s], win_f, 0, 512)
        if own:
            b_g = proj(hT[s], win_g, 0, 512)
        act(sgn[:], b_f[:], AF.Sigmoid, scale=-1.0)
        if own:
            act(sl_[:], b_g[:], AF.Sigmoid)
            tt("dve", sl_[:], b_g[:], sl_[:], ALU.mult)
            tt("pool", slg[:].rearrange("p (h v) -> p h v", h=4), sl_[:].rearrange("p (h v) -> p h v", h=4),
               hgg[:, :].unsqueeze(1).to_broadcast([128, 4, 128]), ALU.mult)
        tt("dve", kk[:], sgn[:], omlb[:], ALU.mult)
        act(gl[:], kk[:], AF.Ln, bias=1.0, scale=-1.0)
        b_D = nb()
        mm(b_D[:], Dmat[:], gl[:], True, True)
        b_s = nb()
        for h in range(4):
            mm(b_s[:, h * 4:(h + 1) * 4], gl[:, h * 128:(h + 1) * 128], Sel[:], True, True)
        b_i = proj(hT[s], win_i, 0, 512)
        cp("dve", vt_[:], b_i[:])
        act(etmp[:], b_D[:], AF.Exp, scale=-1.0)
        tt("dve", kt_[:], kk[:], etmp[:], ALU.mult)
        act(esc[:].rearrange("p h f -> p (h f)"), b_s[:, 0:16], AF.Exp)
        tt("dve", eL[:], esc[:].rearrange("p h (c two) -> p h c two", two=2)[:, :, :, 0],
           esc[:].rearrange("p h (c two) -> p h c two", two=2)[:, :, :, 1], ALU.mult)
        if own:
            b_hq = proj(hT[s], win_hq, 0, 512)
            act(etmp2[:], b_D[:], AF.Exp)
            tt("dve", qt_[:], b_hq[:], etmp2[:], ALU.mult)
            bk2 = nb()
            bv2 = bf(bk2).rearrange("p (c t) -> p c t", c=8)
            for h in range(4):
                tr(bv2[:, h, :], kt_[:, h * 128:(h + 1) * 128], ident[:])
            for h in range(4):
                tr(bv2[:, 4 + h, :], qt_[:, h * 128:(h + 1) * 128], ident[:])
            cp("dve", kqT[:], bv2)
            b_A = nb()
            for h in range(4):
                mm(b_A[:, h * 128:(h + 1) * 128], kqT[:, h, :], kqT[:, 4 + h, :], True, True)
            tt("dve", ATm[:], b_A[:].rearrange("p (h t) -> p h t", h=4),
               mblk[:, :].unsqueeze(1).to_broadcast([128, 4, 128]), ALU.mult)
            b_O = nb()
            for h in range(4):
                mm(b_O[:, h * 128:(h + 1) * 128], ATm[:, h, :], vt_[:, h * 128:(h + 1) * 128], h == 0, False)
        if substage < 3:
            continue
        b_KV = [nb(), nb()]
        for c in range(2):
            for h in range(4):
                mm(b_KV[c][:, h * 128:(h + 1) * 128], kt_[c * 64:(c + 1) * 64, h * 128:(h + 1) * 128],
                   vt_[c * 64:(c + 1) * 64, h * 128:(h + 1) * 128], True, True)
        for c in range(2):
            eA = esc[:, :, 2 * c:2 * c + 1].to_broadcast([128, 4, 128])
            eB = esc[:, :, 2 * c + 1:2 * c + 2].to_broadcast([128, 4, 128])
            eLc = eL[:, :, c:c + 1].to_broadcast([128, 4, 128])
            if own:
                tt("dve", Spb[c][:], Sst[:], eB, ALU.mult)
                for h in range(4):
                    mm(b_O[c * 64:(c + 1) * 64, h * 128:(h + 1) * 128], kqT[:, 4 + h, c * 64:(c + 1) * 64],
                       Spb[c][:, h, :], False, (c == 1 and h == 3))
            tt("pool", Tst[:], Sst[:], eLc, ALU.mult)
            tt("dve", Ust[:], b_KV[c][:].rearrange("p (h v) -> p h v", h=4), eA, ALU.mult)
            tt("dve", Sst[:], Tst[:], Ust[:], ALU.add)
        if substage < 4:
            continue
        if own:
            act(osq[:], b_O[:], AF.Square)
            rsum(oss[:], osq[:].rearrange("p (h v) -> p h v", h=4))
            rstd_of(oss[:], osr[:], ors[:], 128)
            tt("dve", o1[:].rearrange("p (h v) -> p h v", h=4), b_O[:].rearrange("p (h v) -> p h v", h=4),
               ors[:, :].unsqueeze(2).to_broadcast([128, 4, 128]), ALU.mult)
            tt("dve", rec[:], o1[:], slg[:], ALU.mult)
            bk3 = nb()
            bv3 = bf(bk3).rearrange("p (c t) -> p c t", c=8)
            for h in range(4):
                tr(bv3[:, h, :], rec[:, h * 128:(h + 1) * 128], ident[:])
            cp("dve", recT[:, :, o * 128:(o + 1) * 128], bv3[:, 0:4, :])
    dump("recT", recT[:].rearrange("p a b -> p (a b)"))
    dump("cqnT", cqnT[:].rearrange("p a b -> p (a b)"))
    dump("ckvnT", ckvnT[:].rearrange("p a b -> p (a b)"))
    dump("Sst", Sst[:].rearrange("p a b -> p (a b)"))
    if stage == "A":
        P.muted = True
    P.barrier()
    esA.close()

    esB = ExitStack()
    KT = sb(esB, "KT", [96, 4, 4096], BF16)
    Vt = sb(esB, "Vt", [128, 32, 4, 65], BF16)
    NKB = 3
    kvsq2 = [sb(esB, f"kvsq{i}", [128, 512]) for i in range(NKB)]
    kss2 = [sb(esB, f"kss{i}", [128, 4]) for i in range(NKB)]
    ksq2 = [sb(esB, f"ksq{i}", [128, 4]) for i in range(NKB)]
    krs2 = [sb(esB, f"krs{i}", [128, 4]) for i in range(NKB)]
    kfull2 = [sb(esB, f"kfull{i}", [128, 4, 96], BF16) for i in range(NKB)]
    kr42 = [sb(esB, f"kr4{i}", [128, 4, 32]) for i in range(NKB)]
    ra2 = [sb(esB, f"ra{i}", [128, 4, 16]) for i in range(NKB)]
    rb2 = [sb(esB, f"rb{i}", [128, 4, 16]) for i in range(NKB)]
    QT = [sb(esB, f"QT{i}", [96, 4, 512], BF16) for i in range(2)]
    PT = [sb(esB, f"PT{i}", [128, 512], BF16) for i in range(4)]
    dn = sb(esB, "dn", [128, 512])
    rden = sb(esB, "rden", [64, 512])
    SCALE = float(96 ** -0.5)

    def rope(eng, dst, src, t, nh, ra, rb):
        sin_b = cs[:, t, 0:16].unsqueeze(1).to_broadcast([128, nh, 16])
        cos_b = cs[:, t, 16:32].unsqueeze(1).to_broadcast([128, nh, 16])
        x1, x2 = src[:, :, 0:16], src[:, :, 16:32]
        tt(eng, ra[:, 0:nh, :], x1, cos_b, ALU.mult)
        tt(eng, rb[:, 0:nh, :], x2, sin_b, ALU.mult)
        tt(eng, dst[:, :, 64:80], ra[:, 0:nh, :], rb[:, 0:nh, :], ALU.subtract)
        tt(eng, ra[:, 0:nh, :], x2, cos_b, ALU.mult)
        tt(eng, rb[:, 0:nh, :], x1, sin_b, ALU.mult)
        tt(eng, dst[:, :, 80:96], ra[:, 0:nh, :], rb[:, 0:nh, :], ALU.add)

    for half in range(2):
        for t in range(NT):
            u = t % 3
            bk = nb()
            for c in range(2):
                mm(bk[:], ckvnT[:, c, t * 128:(t + 1) * 128], wukv[:, c, half * 512:(half + 1) * 512], c == 0, c == 1)
            kv3 = bk[:].rearrange("p (h d) -> p h d", h=4)
            sq3 = kvsq2[u][:].rearrange("p (h d) -> p h d", h=4)
            act(sq3[:, :, 0:64], kv3[:, :, 0:64], AF.Square)
            rsum(kss2[u][:, 0:4], sq3[:, :, 0:64])
            ts("dve", kss2[u][:, 0:4], kss2[u][:, 0:4], krss_all[:, t:t + 1], None, ALU.add)
            rstd_of(kss2[u][:, 0:4], ksq2[u][:, 0:4], krs2[u][:, 0:4], 96)
            kf = kfull2[u]
            tt("dve", kf[:, :, 0:64], kv3[:, :, 0:64], krs2[u][:, 0:4].unsqueeze(2).to_broadcast([128, 4, 64]), ALU.mult)
            tt("dve", kf[:, :, 64:96], kropeR[:, t, :].unsqueeze(1).to_broadcast([128, 4, 32]),
               krs2[u][:, 0:4].unsqueeze(2).to_broadcast([128, 4, 32]), ALU.mult)
            bk2 = nb()
            bv2 = bf(bk2).rearrange("p (c t) -> p c t", c=8)
            for h in range(4):
                tr(bv2[0:96, h, :], kf[:, h, :], ident[:])
            act(KT[:, :, t * 128:(t + 1) * 128], bv2[0:96, 0:4, :], AF.Identity, scale=kgcol[:, 0:1])
            act(Vt[:, t, :, 0:64], kv3[:, :, 64:128], AF.Identity, scale=kval[:, t:t + 1])
            cp("pool", Vt[:, t, :, 64:65], kval[:, t:t + 1].unsqueeze(1).to_broadcast([128, 4, 1]))
        for g in range(4):
            QTg = QT[g % 2]
            for i in range(4):
                o = g * 4 + i
                t = OWN0 + o
                u = o % 3
                bk = nb((3, 4))
                for c in range(3):
                    mm(bk[:, 0:384], cqnT[:, c, o * 128:(o + 1) * 128], wuq[:, c, half * 384:(half + 1) * 384],
                       c == 0, c == 2)
                q3 = bk[:, 0:384].rearrange("p (h d) -> p h d", h=4)
                act(kvsq2[u][:, 0:384], bk[:, 0:384], AF.Square)
                rsum(kss2[u][:, 0:4], kvsq2[u][:, 0:384].rearrange("p (h d) -> p h d", h=4))
                rstd_of(kss2[u][:, 0:4], ksq2[u][:, 0:4], krs2[u][:, 0:4], 96)
                kf = kfull2[u]
                tt("dve", kf[:, :, 0:64], q3[:, :, 0:64], krs2[u][:, 0:4].unsqueeze(2).to_broadcast([128, 4, 64]), ALU.mult)
                tt("dve", kr42[u][:], q3[:, :, 64:96], krs2[u][:, 0:4].unsqueeze(2).to_broadcast([128, 4, 32]), ALU.mult)
                tt("pool", kr42[u][:], kr42[u][:], qng[:, 64:96].unsqueeze(1).to_broadcast([128, 4, 32]), ALU.mult)
                rope("pool", kf, kr42[u], t, 4, ra2[u], rb2[u])
                bk2 = nb((3, 4))
                bv2 = bf(bk2).rearrange("p (c t) -> p c t", c=8)
                for h in range(4):
                    tr(bv2[0:96, h, :], kf[:, h, :], ident[:])
                act(QTg[:, :, i * 128:(i + 1) * 128], bv2[0:96, 0:4, :], AF.Identity, scale=qgcol[:, 0:1])
            nkt = OWN0 + 4 * g + 4
            for h in range(4):
                hh = half * 4 + h
                b_o = banks[6 + (h % 2)]
                pend = []
                for it in range(nkt + 2):
                    if it < nkt:
                        kt = it
                        j = kt - (OWN0 + 4 * g)
                        qlo = j * 128 if j > 0 else 0
                        b_sc = banks[it % 3]
                        mm(b_sc[:, qlo:512], KT[:, h, kt * 128:(kt + 1) * 128], QTg[:, h, qlo:512], True, True)
                        pt = PT[it % 4]
                        act(pt[:, qlo:512], b_sc[:, qlo:512], AF.Exp, scale=SCALE)
                        if j >= 0:
                            tt("pool", pt[:, qlo:qlo + 128], pt[:, qlo:qlo + 128], maskT[:], ALU.mult)
                        pend.append((kt, qlo, pt))
                    if it >= 2:
                        kt, qlo, pt = pend[it - 2]
                        mm(b_o[0:65, qlo:512], Vt[:, kt, h, :], pt[:, qlo:512], kt == 0, kt == nkt - 1)
                cp("dve", dn[64:65, :], b_o[64:65, :])
                b_d = banks[5]
                mm(b_d[0:64, :], ones_f[64:65, 0:64], dn[64:65, :], True, True)
                recip(rden[:], b_d[0:64, :])
                po = (hh % 2) * 64
                tt("dve", attnT[po:po + 64, hh // 2, g * 512:(g + 1) * 512], b_o[0:64, :], rden[:], ALU.mult)
    dump("attnT", attnT[:].rearrange("p a b -> p (a b)"))
    dump("KT", KT[:].rearrange("p a b -> p (a b)"))
    if stage == "B":
        P.muted = True
    P.barrier()
    esB.close()
    es1.close()

    esC = ExitStack()
    wg = wload(esC, "wg", w_in, 8, C_GT, 2048, "wg", split=4)
    wb = wload(esC, "wb", w_branch, 8, 0, 1024, "wb", split=2)
    xc = [sb(esC, f"xc{i}", [128, 1024]) for i in range(2)]
    junkc = sb(esC, "junkc", [128, 1024], BF16)
    hnc = sb(esC, "hnc", [128, 1024], BF16)
    hTc = [sb(esC, f"hTc{i}", [128, 8, 128], BF16) for i in range(2)]
    smc = [dict(ss=sb(esC, f"css{i}", [128, 4]), sq=sb(esC, f"csq{i}", [128, 4]), rs=sb(esC, f"crs{i}", [128, 4]))
           for i in range(2)]
    gts = sb(esC, "gts", [128, 2048])
    m1 = sb(esC, "m1", [128, 512])
    m2 = sb(esC, "m2", [128, 512])
    mtok = sb(esC, "mtok", [128, 1024], BF16)

    for o in range(NOWN):
        t = OWN0 + o
        s = o % 2
        dma("sync", xc[s][:], xw[t * 128:(t + 1) * 128, :], f"xc{s}")
        norm_tile_to_hT(xc[s][:], smc[s], hTc[s], mixg, junkc, hnc)
        for gq in range(4):
            bk = proj(hTc[s], wg, gq * 512, 512)
            act(gts[:, gq * 512:(gq + 1) * 512], bk[:], AF.Sigmoid)
        for hf in range(2):
            b_a = nb()
            for j in range(4):
                mm(b_a[:], attnT[:, j, o * 128:(o + 1) * 128], wb[:, j, hf * 512:(hf + 1) * 512], j == 0, j == 3)
            b_r = nb()
            for j in range(4):
                mm(b_r[:], recT[:, j, o * 128:(o + 1) * 128], wb[:, 4 + j, hf * 512:(hf + 1) * 512], j == 0, j == 3)
            tt("dve", m1[:], b_a[:], gts[:, hf * 512:(hf + 1) * 512], ALU.mult)
            tt("dve", m2[:], b_r[:], gts[:, 1024 + hf * 512:1024 + (hf + 1) * 512], ALU.mult)
            tt("pool", mtok[:, hf * 512:(hf + 1) * 512], m1[:], m2[:], ALU.add)
        bk = nb()
        bv = bf(bk).rearrange("p (c t) -> p c t", c=8)
        for c in range(8):
            tr(bv[:, c, :], mtok[:, c * 128:(c + 1) * 128], ident[:])
        cp("dve", attnT[:, :, o * 128:(o + 1) * 128], bv[:, 0:4, :])
        cp("dve", recT[:, :, o * 128:(o + 1) * 128], bv[:, 4:8, :])
    P.barrier()
    esC.close()

    es2 = ExitStack()
    x1 = sb(es2, "x1", [128, 16, 1024])
    h2T = sb(es2, "h2T", [128, 8, 2048], BF16)
    esC2 = ExitStack()
    wo = wload(esC2, "wo", w_out, 8, 0, 1024, "wo", split=2)
    xd = [sb(esC2, f"xd{i}", [128, 1024]) for i in range(2)]
    junkd = sb(esC2, "junkd", [128, 1024], BF16)
    hnd = sb(esC2, "hnd", [128, 1024], BF16)
    smc2 = [dict(ss=sb(esC2, f"dss{i}", [128, 4]), sq=sb(esC2, f"dsq{i}", [128, 4]), rs=sb(esC2, f"drs{i}", [128, 4]))
            for i in range(2)]
    h2s = sb(esC2, "h2s", [128, 8, 128], BF16)
    for o in range(NOWN):
        t = OWN0 + o
        s = o % 2
        dma("sync", xd[s][:], xw[t * 128:(t + 1) * 128, :], f"xd{s}")
        for hf in range(2):
            b_z = nb()
            for c in range(8):
                src = attnT if c < 4 else recT
                mm(b_z[:], src[:, c % 4, o * 128:(o + 1) * 128], wo[:, c, hf * 512:(hf + 1) * 512], c == 0, c == 7)
            tt("dve", x1[:, o, hf * 512:(hf + 1) * 512], b_z[:], xd[s][:, hf * 512:(hf + 1) * 512], ALU.add)
        norm_tile_to_hT(x1[:, o, :], smc2[s], h2s, ffng, junkd, hnd)
        cp("pool", h2T[:, :, o * 128:(o + 1) * 128], h2s[:])
    dump("x1", x1[:].rearrange("p a b -> p (a b)"))
    if stage == "C":
        P.muted = True
    P.barrier()
    esC2.close()

    esD = ExitStack()
    NBUF = 3
    hidT2 = [sb(esD, f"hidT{i}", [128, 2, 512], BF16) for i in range(2)]
    sg = [sb(esD, f"sg{i}", [128, 512]) for i in range(2)]
    wfgb = [sb(esD, f"wfgb{i}", [128, 8, 256], BF16) for i in range(NBUF)]
    wfub = [sb(esD, f"wfub{i}", [128, 8, 256], BF16) for i in range(NBUF)]
    wfdb = [sb(esD, f"wfdb{i}", [128, 2, 1024], BF16) for i in range(NBUF)]
    vfg = w_fg.rearrange("(c p) n -> p c n", p=128)
    vfu = w_fu.rearrange("(c p) n -> p c n", p=128)
    vfd = w_fd.rearrange("(c p) n -> p c n", p=128)
    NGRP = 11

    def ffn_load(gi):
        i = gi % NBUF
        for c in range(8):
            dma("pool", wfgb[i][:, c, :], vfg[:, c, gi * 256:(gi + 1) * 256], f"fg{i}", nowaw=True, max_dma_last_dim=4096)
        for c in range(8):
            dma("pool", wfub[i][:, c, :], vfu[:, c, gi * 256:(gi + 1) * 256], f"fu{i}", nowaw=True, max_dma_last_dim=4096)
        for c0 in range(2):
            dma("pool", wfdb[i][:, c0, :], vfd[:, gi * 2 + c0, :], f"fd{i}", nowaw=True, max_dma_last_dim=4096)

    ffn_load(0)
    ffn_load(1)
    cnt = 0
    for gi in range(NGRP):
        if gi + 2 < NGRP:
            ffn_load(gi + 2)
        i = gi % NBUF
        for tb in range(4):
            hid = hidT2[cnt % 2]
            cnt += 1
            for hc in range(2):
                b_g = nb()
                for c in range(8):
                    mm(b_g[:], wfgb[i][:, c, hc * 128:(hc + 1) * 128], h2T[:, c, tb * 512:(tb + 1) * 512], c == 0, c == 7)
                b_u = nb()
                for c in range(8):
                    mm(b_u[:], wfub[i][:, c, hc * 128:(hc + 1) * 128], h2T[:, c, tb * 512:(tb + 1) * 512], c == 0, c == 7)
                sgi = sg[hc % 2]
                act(sgi[:], b_g[:], AF.Silu)
                tt("dve", hid[:, hc, :], b_u[:], sgi[:], ALU.mult)
            for i4 in range(4):
                o = tb * 4 + i4
                for hf in range(2):
                    b_d = nb()
                    for hc in range(2):
                        mm(b_d[:], hid[:, hc, i4 * 128:(i4 + 1) * 128], wfdb[i][:, hc, hf * 512:(hf + 1) * 512],
                           hc == 0, hc == 1)
                    tt("dve", x1[:, o, hf * 512:(hf + 1) * 512], b_d[:], x1[:, o, hf * 512:(hf + 1) * 512], ALU.add)
    dump("x2", x1[:].rearrange("p a b -> p (a b)"))
    if stage == "D":
        P.muted = True
    P.barrier()
    esD.close()

    esE = ExitStack()
    wpg = wload(esE, "wpg", w_pg, 8, 0, 1024, "wpg", split=2)
    wpp = wload(esE, "wpp", w_pp, 2, 0, 1024, "wpp")
    pt_ = [sb(esE, f"pt{i}", [128, 256]) for i in range(2)]
    pb_2 = [sb(esE, f"pb{i}", [128, 256], BF16) for i in range(2)]
    pT2 = [sb(esE, f"pT{i}", [128, 2, 128], BF16) for i in range(2)]
    junke = sb(esE, "junke", [128, 1024], BF16)
    hne2 = [sb(esE, f"hne{i}", [128, 1024], BF16) for i in range(2)]
    h3T2 = [sb(esE, f"h3T{i}", [128, 8, 128], BF16) for i in range(2)]
    sme = [dict(ss=sb(esE, f"ess{i}", [128, 4]), sq=sb(esE, f"esq{i}", [128, 4]), rs=sb(esE, f"ers{i}", [128, 4]))
           for i in range(2)]
    sme2 = [dict(ss=sb(esE, f"fss{i}", [128, 4]), sq=sb(esE, f"fsq{i}", [128, 4]), rs=sb(esE, f"frs{i}", [128, 4]))
            for i in range(2)]
    sgm2 = [sb(esE, f"sgm{i}", [128, 1024]) for i in range(2)]
    e12 = [sb(esE, f"e1{i}", [128, 1024]) for i in range(2)]
    yo = [sb(esE, f"yo{i}", [128, 1024]) for i in range(2)]
    for o in range(NOWN):
        s = o % 2
        pb_, pT, hne, h3T, sgm, e1 = pb_2[s], pT2[s], hne2[s], h3T2[s], sgm2[s], e12[s]
        dma("sync", pt_[s][:], pw[o * 128:(o + 1) * 128, :], f"p{s}")
        cp("pool", pb_[:], pt_[s][:])
        bk = nb()
        bv = bf(bk).rearrange("p (c t) -> p c t", c=8)
        for c in range(2):
            tr(bv[:, c, :], pb_[:, c * 128:(c + 1) * 128], ident[:])
        cp("dve", pT[:], bv[:, 0:2, :])
        b_e = [nb(), nb()]
        sm = sme[s]
        for hf in range(2):
            for c in range(2):
                mm(b_e[hf][:], pT[:, c, :], wpp[:, c, hf * 512:(hf + 1) * 512], c == 0, c == 1)
            act(junke[:, hf * 512:(hf + 1) * 512], b_e[hf][:], AF.Square, accum=sm["ss"][:, hf:hf + 1])
        ts("dve", sm["ss"][:, 2:3], sm["ss"][:, 0:1], sm["ss"][:, 1:2], None, ALU.add)
        rstd_of(sm["ss"][:, 2:3], sm["sq"][:, 2:3], sm["rs"][:, 2:3], 1024)
        norm_tile_to_hT(x1[:, o, :], sme2[s], h3T, plgg, junke, hne)
        for hf in range(2):
            b_g = nb()
            for c in range(8):
                mm(b_g[:], h3T[:, c, :], wpg[:, c, hf * 512:(hf + 1) * 512], c == 0, c == 7)
            act(sgm[:, hf * 512:(hf + 1) * 512], b_g[:], AF.Sigmoid)
            stt(e1[:, hf * 512:(hf + 1) * 512], b_e[hf][:], sm["rs"][:, 2:3], ppg[:, hf * 512:(hf + 1) * 512],
                ALU.mult, ALU.mult)
        tt("pool", e1[:], e1[:], sgm[:], ALU.mult)
        tt("dve", yo[s][:], e1[:], x1[:, o, :], ALU.add)
        dma("sync", y[o * 128:(o + 1) * 128, :], yo[s][:], f"y{s}", writes=[("yout", s)])
    P.muted = False
    P.add("sync", None, reads=[("yout", 0), ("yout", 1)] + [d for d in dbg_out.values()])

    P.finalize()
    chan_names = sorted(P.chan_n.keys())
    with ExitStack() as ess:
        sems = {e: ess.enter_context(nc.semaphore(f"s_{e}")) for e in ("act", "dve", "pool", "pe")}
        chans = {c: ess.enter_context(nc.semaphore(f"c_{c}")) for c in chan_names}
        block = ess.enter_context(nc.Block())

        @block.sync
        def _(e):
            P.emit("sync", e, sems, chans)

        @block.scalar
        def _(e):
            P.emit("act", e, sems, chans)

        @block.vector
        def _(e):
            P.emit("dve", e, sems, chans)

        @block.gpsimd
        def _(e):
            P.emit("pool", e, sems, chans)

        @block.tensor
        def _(e):
            P.emit("pe", e, sems, chans)
    esE.close()
    es2.close()
    es0.close()
    return nc


_NC_CACHE = {}


def _pc(v, k):
    return np.ascontiguousarray(np.asarray(v, np.float32).reshape(k, 128).T)


def kernel(x, p, positions, mix_norm_g, w_in, q_a_norm_g, w_uq, kv_a_norm_g, w_ukv, q_norm_g, k_norm_g,
           hg_lb_logits, hg_out_norm_g, w_branch, w_out, ffn_norm_g, w_ffn_gate, w_ffn_up, w_ffn_down,
           ple_gate_norm_g, w_ple_gate, w_ple_proj, ple_post_norm_g):
    x = np.asarray(x, np.float32)
    p = np.asarray(p, np.float32)
    positions = np.asarray(positions, np.int32)
    f = lambda a: np.ascontiguousarray(np.asarray(a, np.float32))
    shared = {
        "w_in": f(w_in[0]), "w_uq": f(w_uq[0]), "w_ukv": f(w_ukv[0]),
        "w_branch": f(np.asarray(w_branch)[0].reshape(1024, 1024)), "w_out": f(w_out[0]),
        "w_fg": f(w_ffn_gate[0]), "w_fu": f(w_ffn_up[0]), "w_fd": f(w_ffn_down[0]),
        "w_pg": f(w_ple_gate[0]), "w_pp": f(w_ple_proj[0]),
        "mixg": _pc(mix_norm_g[0], 8), "ffng": _pc(ffn_norm_g[0], 8), "plgg": _pc(ple_gate_norm_g[0], 8),
        "qag": _pc(q_a_norm_g[0], 3), "kvag": _pc(kv_a_norm_g[0], 2),
        "qng": f(q_norm_g[0]).reshape(1, 96), "kng": f(k_norm_g[0]).reshape(1, 96),
        "hgg": f(hg_out_norm_g[0]).reshape(1, 128), "ppg": f(ple_post_norm_g[0]).reshape(1, 1024),
        "lbl": f(hg_lb_logits),
    }
    in_maps = []
    for core in range(8):
        b, j = core // 2, core % 2
        if j == 1:
            xwin = x[b]
            pos = positions[b]
            kv = np.ones(4096, np.float32)
        else:
            xwin = np.concatenate([np.zeros((2048, 1024), np.float32), x[b, :2048]], axis=0)
            pos = np.concatenate([np.zeros(2048, np.int32), positions[b, :2048]])
            kv = np.concatenate([np.zeros(2048, np.float32), np.ones(2048, np.float32)])
        m = dict(shared)
        m["xw"] = np.ascontiguousarray(xwin)
        m["pw"] = np.ascontiguousarray(p[0, b, j * 2048:(j + 1) * 2048])
        m["posw"] = np.ascontiguousarray(pos.reshape(32, 128).T.astype(np.int32))
        m["kvalw"] = np.ascontiguousarray(kv.reshape(32, 128).T)
        in_maps.append(m)
    if _NC_CACHE.get("maps_only"):
        return in_maps
    if "nc" not in _NC_CACHE:
        _NC_CACHE["nc"] = build_nc()
    res = run_bass_kernel_spmd(_NC_CACHE["nc"], in_maps, core_ids=list(range(8)))
    out = np.empty((4, 4096, 1024), np.float32)
    for core in range(8):
        b, j = core // 2, core % 2
        out[b, j * 2048:(j + 1) * 2048] = res.results[core]["y"]
    return out
```

```python
import numpy as np
from contextlib import ExitStack
import concourse.bass as bass
import concourse.mybir as mybir
from concourse.bass_utils import run_bass_kernel_spmd

F32, BF16, I32 = mybir.dt.float32, mybir.dt.bfloat16, mybir.dt.int32
ALU = mybir.AluOpType
AF = mybir.ActivationFunctionType
AX = mybir.AxisListType

EPS = 1e-6
NT = 32
OWN0 = 16
NOWN = 16
C_CQ, C_CKV, C_KR, C_HQ, C_HF, C_HI, C_HG, C_GT = 0, 384, 640, 672, 1184, 1696, 2208, 2720
NG_COLS = 2720
FFN_H = 2816
TWO_PI = float(2 * np.pi)
CW1 = 6.28125
CW2 = float(2 * np.pi - 6.28125)


class _Op:
    __slots__ = ("eng", "fn", "deps", "signal", "sigval", "chan", "wsz", "odeps", "cost", "lat", "idx", "aset")


class Prog:
    ENGS = ("sync", "act", "dve", "pool", "pe")

    def __init__(self):
        self.ops = {e: [] for e in self.ENGS}
        self.lastw = {}
        self.readers = {}
        self.chan_n = {}
        self.muted = False
        self.seg = []
        self.nops = 0
        self.do_sched = True
        self.prio_mode = "cp"

    @staticmethod
    def _keys(items):
        out = []
        for it in items:
            if it is None:
                continue
            if isinstance(it, (str, tuple)):
                out.append(it)
            elif hasattr(it, "tensor"):
                out.append(it.tensor.name)
            else:
                out.append(it.name)
        return out

    def add(self, eng, fn, reads=(), writes=(), sreads=(), chan=None, cost=0.1, lat=0.0, nowaw=False):
        if self.muted:
            return None
        op = _Op()
        op.eng, op.fn, op.deps, op.signal, op.sigval, op.chan = eng, fn, {}, False, 0, chan
        op.odeps, op.cost, op.lat, op.idx = {}, cost, lat, self.nops
        op.aset = None
        self.nops += 1
        op.wsz = 1 << 30
        for w_ in writes:
            if w_ is not None and hasattr(w_, "shape") and hasattr(w_, "tensor"):
                n_ = 1
                for d_ in list(w_.shape)[1:]:
                    n_ *= int(d_)
                op.wsz = min(op.wsz, n_)
        if chan is not None:
            self.chan_n[chan] = self.chan_n.get(chan, 0) + 1
            op.sigval = 16 * self.chan_n[chan]

        def dep(o, force=False):
            if o is None or o is op:
                return
            if o.eng == eng and o.chan is None and not force:
                op.odeps[id(o)] = o
                return
            op.deps[id(o)] = o

        rk, sk, wk = self._keys(reads), self._keys(sreads), self._keys(writes)
        wk = wk + [k for k in rk if isinstance(k, str) and k.startswith("bank")]
        rk = [k for k in rk if not (isinstance(k, str) and k.startswith("bank"))]
        for k in rk:
            lw = self.lastw.get(k)
            dep(lw, lw is not None and lw.eng == eng and lw.wsz <= 256)
        for k in sk:
            dep(self.lastw.get(k), True)
        for k in wk:
            lw = self.lastw.get(k)
            if nowaw and lw is not None and lw.chan is not None and lw.chan == chan:
                for o_ in lw.deps.values():
                    dep(o_)
            else:
                dep(lw)
            for r in self.readers.get(k, ()):
                dep(r)
        for k in wk:
            self.lastw[k] = op
            self.readers[k] = []
        for k in rk + sk:
            self.readers.setdefault(k, []).append(op)
        self.seg.append(op)
        return op

    def flush_segment(self):
        import heapq
        ops = self.seg
        self.seg = []
        if not ops:
            return
        if not self.do_sched:
            for op in ops:
                self.ops[op.eng].append(op)
            return
        SYNC_ = 1.0
        PRIO = self.prio_mode
        inseg = {id(o) for o in ops}
        preds = {}
        succs = {id(o): [] for o in ops}
        npred = {}
        for o in ops:
            pl = [p for p in list(o.deps.values()) + list(o.odeps.values()) if id(p) in inseg]
            preds[id(o)] = pl
            npred[id(o)] = len(pl)
            for p in pl:
                succs[id(p)].append(o)
        bl = {}
        for o in reversed(ops):
            m_ = 0.0
            for sc in succs[id(o)]:
                v_ = bl[id(sc)] + (SYNC_ if sc.eng != o.eng else 0.0)
                if v_ > m_:
                    m_ = v_
            bl[id(o)] = o.cost + o.lat + m_
        finish = {}
        endt = {}
        free = {e: 0.0 for e in self.ENGS}
        pending = {e: [] for e in self.ENGS}
        ready = {e: [] for e in self.ENGS}
        SYNC = 1.0

        def make_eligible(o):
            dr = 0.0
            for p in preds[id(o)]:
                if p.eng != o.eng or p.chan is not None:
                    t = finish[id(p)] + SYNC
                else:
                    t = endt[id(p)]
                if t > dr:
                    dr = t
            heapq.heappush(pending[o.eng], (dr, o.idx, o))

        for o in ops:
            if npred[id(o)] == 0:
                make_eligible(o)
        out = {e: [] for e in self.ENGS}
        n_done = 0
        cur_set = [None]
        while n_done < len(ops):
            best = None
            for e in self.ENGS:
                pe_, re_ = pending[e], ready[e]
                while pe_ and pe_[0][0] <= free[e]:
                    dr, ix, o = heapq.heappop(pe_)
                    heapq.heappush(re_, ((-bl[id(o)] if PRIO == "cp" else ix), ix, dr, o))
                if re_:
                    st = free[e]
                elif pe_:
                    st = pe_[0][0]
                else:
                    continue
                if best is None or st < best[0]:
                    best = (st, e)
            st, e = best
            if ready[e]:
                if e == "act" and len(ready[e]) > 1:
                    cand = [heapq.heappop(ready[e]) for _ in range(min(len(ready[e]), 6))]
                    pick = 0
                    for ci, (pk_, ix_, dr_, o_) in enumerate(cand):
                        if o_.aset is None or o_.aset == cur_set[0]:
                            pick = ci
                            break
                    pk_, ix, dr, o = cand.pop(pick)
                    for c_ in cand:
                        heapq.heappush(ready[e], c_)
                else:
                    pk_, ix, dr, o = heapq.heappop(ready[e])
            else:
                dr, ix, o = heapq.heappop(pending[e])
            if e == "act" and o.aset is not None:
                if cur_set[0] is not None and cur_set[0] != o.aset:
                    free[e] += 1.3
                cur_set[0] = o.aset
            start = max(free[e], dr)
            free[e] = start + o.cost
            finish[id(o)] = start + o.cost + o.lat
            endt[id(o)] = start + o.cost
            out[e].append(o)
            n_done += 1
            for sc in succs[id(o)]:
                npred[id(sc)] -= 1
                if npred[id(sc)] == 0:
                    make_eligible(sc)
        for e in self.ENGS:
            self.ops[e].extend(out[e])
        self.seg_time = getattr(self, "seg_time", []) + [max(free.values())]

    def barrier(self):
        if self.muted:
            return
        self.flush_segment()
        lasts = {}
        for e in self.ENGS:
            for o in reversed(self.ops[e]):
                if o.fn is not None:
                    lasts[e] = o
                    break
        for e in self.ENGS:
            op = _Op()
            op.eng, op.fn, op.signal, op.sigval, op.chan, op.wsz = e, None, False, 0, None, 1 << 30
            op.aset = None
            op.deps = {id(o): o for ee, o in lasts.items() if ee != e}
            self.ops[e].append(op)

    def finalize(self):
        self.flush_segment()
        for e in self.ENGS:
            for op in self.ops[e]:
                for o in op.deps.values():
                    o.signal = True
        for e in self.ENGS:
            n = 0
            for op in self.ops[e]:
                if op.chan is None and op.signal:
                    n += 1
                    op.sigval = n

    def emit(self, eng_name, eng, sems, chans):
        waited = {}
        for op in self.ops[eng_name]:
            for o in op.deps.values():
                sem = chans[o.chan] if o.chan is not None else sems[o.eng]
                key = id(sem)
                if waited.get(key, 0) >= o.sigval:
                    continue
                eng.wait_ge(sem, o.sigval)
                waited[key] = o.sigval
            if op.fn is None:
                continue
            ins = op.fn(eng)
            if op.chan is not None:
                ins.then_inc(chans[op.chan], 16)
            elif op.signal:
                ins.then_inc(sems[eng_name], 1)


def build_nc(stage="Z", dbg=None, a_tiles=None, substage=9):
    nc = bass.Bass("TRN2", target_bir_lowering=False)
    P = Prog()
    dbg_specs = []

    def din(name, shape, dt=F32):
        return nc.dram_tensor(name, shape, dt, kind="ExternalInput").ap()

    xw = din("xw", [4096, 1024])
    pw = din("pw", [2048, 256])
    posw = din("posw", [128, 32], I32)
    kvalw = din("kvalw", [128, 32])
    w_in = din("w_in", [1024, 4768])
    w_uq = din("w_uq", [384, 768])
    w_ukv = din("w_ukv", [256, 1024])
    w_branch = din("w_branch", [1024, 1024])
    w_out = din("w_out", [1024, 1024])
    w_fg = din("w_fg", [1024, FFN_H])
    w_fu = din("w_fu", [1024, FFN_H])
    w_fd = din("w_fd", [FFN_H, 1024])
    w_pg = din("w_pg", [1024, 1024])
    w_pp = din("w_pp", [256, 1024])
    mixg_d = din("mixg", [128, 8])
    ffng_d = din("ffng", [128, 8])
    plgg_d = din("plgg", [128, 8])
    qag_d = din("qag", [128, 3])
    kvag_d = din("kvag", [128, 2])
    qng_d = din("qng", [1, 96])
    kng_d = din("kng", [1, 96])
    hgg_d = din("hgg", [1, 128])
    ppg_d = din("ppg", [1, 1024])
    lbl_d = din("lbl", [2, 512])
    y = nc.dram_tensor("y", [2048, 1024], F32, kind="ExternalOutput").ap()

    es0 = ExitStack()
    dbg_out = {}

    def dump(name, ap):
        if dbg is None or name not in dbg:
            return
        shp = list(ap.shape)
        d = nc.dram_tensor("dbg_" + name, shp, F32, kind="ExternalOutput").ap()
        dbg_out[name] = d
        P.add("pool", lambda e: e.dma_start(out=d, in_=ap, max_dma_last_dim=2048), reads=[ap], writes=[d, ("yout", 0)],
              chan="dbg_" + name)

    def sb(es, name, shape, dt=F32):
        return es.enter_context(nc.sbuf_tensor(name, shape, dt))

    banks = [es0.enter_context(nc.psum_tensor(f"bank{i}", [128, 512], F32)) for i in range(8)]

    def _n(ap):
        n = 1
        for d_ in list(ap.shape)[1:]:
            n *= int(d_)
        return n

    def dma(q, out, in_, chan, reads=(), writes=(), nowaw=False, **kw):
        nbytes = _n(out) * int(out.shape[0]) * 4
        P.add(q, lambda e: e.dma_start(out=out, in_=in_, **kw), reads=[in_] + list(reads),
              writes=[out] + list(writes), chan=chan, cost=0.15, lat=2.0 + nbytes / 120e3, nowaw=nowaw)

    def act(out, in_, func, bias=0.0, scale=1.0, accum=None, extra_r=(), sr=()):
        srl = list(sr)
        for v in (bias, scale):
            if not isinstance(v, (int, float)):
                srl.append(v)
        kw = {}
        if accum is not None:
            kw["accum_out"] = accum
        op_ = P.add("act", lambda e: e.activation(out=out, in_=in_, func=func, bias=bias, scale=scale, **kw),
                    reads=[in_] + list(extra_r), writes=[out, accum], sreads=srl,
                    cost=0.22 + _n(out) / 1.2e3 + (0.1 if accum is not None else 0.0))
        if op_ is not None:
            op_.aset = {AF.Exp: "le", AF.Ln: "le", AF.Sigmoid: "sg", AF.Silu: "si", AF.Sin: "tr", AF.Sqrt: "sq"}.get(func)

    def tt(eng, out, in0, in1, op):
        P.add(eng, lambda e: e.tensor_tensor(out=out, in0=in0, in1=in1, op=op), reads=[in0, in1], writes=[out],
              cost=(0.12 + _n(out) / 0.96e3) if eng == "dve" else (0.3 + _n(out) / 0.45e3))

    def ts(eng, out, in0, s1, s2, op0, op1=None):
        srl = [v for v in (s1, s2) if v is not None and not isinstance(v, (int, float))]
        if op1 is None:
            P.add(eng, lambda e: e.tensor_scalar(out=out, in0=in0, scalar1=s1, scalar2=None, op0=op0),
                  reads=[in0], writes=[out], sreads=srl, cost=0.12 + _n(out) / 0.96e3)
        else:
            P.add(eng, lambda e: e.tensor_scalar(out=out, in0=in0, scalar1=s1, scalar2=s2, op0=op0, op1=op1),
                  reads=[in0], writes=[out], sreads=srl, cost=0.12 + _n(out) / 0.96e3)

    def stt(out, in0, scalar, in1, op0, op1):
        srl = [scalar] if not isinstance(scalar, (int, float)) else []
        P.add("dve", lambda e: e.scalar_tensor_tensor(out=out, in0=in0, scalar=scalar, in1=in1, op0=op0, op1=op1),
              reads=[in0, in1], writes=[out], sreads=srl, cost=0.12 + _n(out) / 0.96e3)

    def cp(eng, out, in_):
        P.add(eng, lambda e: e.tensor_copy(out=out, in_=in_), reads=[in_], writes=[out],
              cost=(0.12 + _n(out) / 0.96e3) if eng == "dve" else (0.3 + _n(out) / 0.45e3))

    def recip(out, in_):
        P.add("dve", lambda e: e.reciprocal(out=out, in_=in_), reads=[in_], writes=[out], cost=0.12 + _n(out) / 0.96e3)

    def rsum(out, in_):
        P.add("dve", lambda e: e.reduce_sum(out=out, in_=in_, axis=AX.X), reads=[in_], writes=[out],
              cost=0.12 + _n(in_) / 0.96e3)

    def memset(eng, ap, val):
        P.add(eng, lambda e: e.memset(ap, val), writes=[ap], cost=0.1 + _n(ap) / 0.96e3)

    def mm(out, lhsT, rhs, start, stop, extra_r=()):
        f32 = (rhs.dtype == F32)
        P.add("pe", lambda e: e.matmul(out, lhsT=lhsT, rhs=rhs, start=start, stop=stop, skip_group_check=True),
              reads=[lhsT, rhs] + list(extra_r), writes=[out],
              cost=(0.02 + max(_n(rhs), 64) / 2.4e3) * (4 if f32 else 1), lat=0.1)

    def tr(out, in_, ident_ap):
        P.add("pe", lambda e: e.transpose(out=out, in_=in_, identity=ident_ap), reads=[in_, ident_ap], writes=[out],
              cost=0.1, lat=0.1)

    def asel(out, in_, pattern, cmp, fill, base, cm):
        P.add("pool", lambda e: e.affine_select(out=out, in_=in_, pattern=pattern, compare_op=cmp, fill=fill,
                                                base=base, channel_multiplier=cm), reads=[in_], writes=[out])

    def wload(es, name, src, kch, col0, ncols, chan, split=1):
        t = sb(es, name, [128, kch, ncols], BF16)
        v = src.rearrange("(c p) n -> p c n", p=128)
        for c0 in range(kch):
            dma("pool", t[:, c0, :], v[:, c0, col0:col0 + ncols], chan, nowaw=True, max_dma_last_dim=4096)
        return t

    ident = sb(es0, "ident", [128, 128], BF16)
    maskT = sb(es0, "maskT", [128, 128], BF16)
    mblk = sb(es0, "mblk", [128, 128], F32)
    Dmat = sb(es0, "Dmat", [128, 128], F32)
    Sel = sb(es0, "Sel", [128, 4], F32)
    ones_f = sb(es0, "ones_f", [128, 128], F32)
    mixg = sb(es0, "mixg_s", [128, 8])
    ffng = sb(es0, "ffng_s", [128, 8])
    plgg = sb(es0, "plgg_s", [128, 8])
    qag = sb(es0, "qag_s", [128, 3])
    kvag = sb(es0, "kvag_s", [128, 2])
    qng = sb(es0, "qng_s", [128, 96])
    kng = sb(es0, "kng_s", [128, 96])
    hgg = sb(es0, "hgg_s", [128, 128])
    ppg = sb(es0, "ppg_s", [128, 1024])
    omlb = sb(es0, "omlb", [128, 512])
    posi = sb(es0, "posi", [128, 32], I32)
    posf = sb(es0, "posf", [128, 32])
    kval = sb(es0, "kval", [128, 32])
    cs = sb(es0, "cs", [128, 32, 32])
    recT = sb(es0, "recT", [128, 4, 2048], BF16)
    attnT = sb(es0, "attnT", [128, 4, 2048], BF16)
    esS = ExitStack()
    csn = sb(esS, "csn", [128, 32, 32])
    csi = sb(esS, "csi", [128, 32, 32], I32)
    lb0 = sb(esS, "lb0", [128, 512])
    lb1 = sb(esS, "lb1", [128, 512])
    invf = sb(esS, "invf", [128, 32])
    offs = sb(esS, "offs", [128, 32])
    identf = sb(esS, "identf", [128, 128], F32)
    gecol = sb(esS, "gecol", [128, 4], F32)

    for i, (dst, src) in enumerate([(mixg, mixg_d), (ffng, ffng_d), (plgg, plgg_d), (qag, qag_d), (kvag, kvag_d),
                                    (posi, posw), (kval, kvalw)]):
        dma("sync", dst[:], src[:, :], f"su{i}")
    for i, (dst, src) in enumerate([(qng, qng_d), (kng, kng_d), (hgg, hgg_d), (ppg, ppg_d),
                                    (lb0, lbl_d[0:1, :]), (lb1, lbl_d[1:2, :])]):
        dma("sync", dst[:], src.partition_broadcast(128), f"sb{i}")

    memset("pool", ones_f[:], 1.0)
    asel(identf[:], ones_f[:], [[-1, 128]], ALU.is_equal, 0.0, 0, 1)
    cp("pool", ident[:], identf[:])
    asel(mblk[:], ones_f[:], [[1, 128]], ALU.is_ge, 0.0, 0, -1)
    cp("pool", maskT[:], mblk[:])
    memset("pool", mblk[0:64, 64:128], 0.0)
    for i in range(4):
        asel(gecol[:, i:i + 1], ones_f[:, 0:1], [[0, 1]], ALU.is_ge, 0.0, -32 * i, 1)
    tt("pool", Sel[:, 0:1], gecol[:, 1:2], gecol[:, 2:3], ALU.subtract)
    tt("pool", Sel[:, 1:2], gecol[:, 0:1], gecol[:, 1:2], ALU.subtract)
    cp("pool", Sel[:, 2:3], gecol[:, 3:4])
    tt("pool", Sel[:, 3:4], gecol[:, 2:3], gecol[:, 3:4], ALU.subtract)
    tt("pool", Dmat[:, 0:64], mblk[:, 0:64], Sel[:, 1:2].to_broadcast([128, 64]), ALU.subtract)
    tt("pool", Dmat[:, 64:128], mblk[:, 64:128], Sel[:, 3:4].to_broadcast([128, 64]), ALU.subtract)
    tt("dve", omlb[:], lb1[:], lb0[:], ALU.subtract)
    act(omlb[:], omlb[:], AF.Sigmoid)
    for i in range(16):
        f = float(np.exp(-np.log(10000.0) * i * 2.0 / 32))
        memset("pool", invf[:, i:i + 1], f)
        memset("pool", invf[:, 16 + i:17 + i], f)
    memset("pool", offs[:, 0:16], 0.0)
    memset("pool", offs[:, 16:32], float(np.pi / 2))
    cp("dve", posf[:], posi[:])
    for t in range(NT):
        stt(cs[:, t, :], invf[:], posf[:, t:t + 1], offs[:], ALU.mult, ALU.add)
    ts("dve", csn[:], cs[:], 1.0 / TWO_PI, None, ALU.mult)
    cp("dve", csi[:], csn[:])
    cp("dve", csn[:], csi[:])
    stt(cs[:], csn[:], -CW1, cs[:], ALU.mult, ALU.add)
    stt(cs[:], csn[:], -CW2, cs[:], ALU.mult, ALU.add)
    ts("dve", cs[:], cs[:], float(np.pi), float(-np.pi), ALU.min, ALU.max)
    act(cs[:], cs[:], AF.Sin)
    dump("cs", cs[:].rearrange("p a b -> p (a b)"))
    dump("Dmat", Dmat[:])
    dump("Sel", Sel[:])
    dump("omlb", omlb[:])
    dump("mblk", mblk[:])
    if stage == "S":
        P.muted = True

    bank_rr = [0]

    def nb(pool=None):
        if pool is None:
            pool = (0, 1, 2, 3, 4, 5, 6, 7)
        b = banks[pool[bank_rr[0] % len(pool)]]
        bank_rr[0] += 1
        return b

    def bf(bank):
        return bank[:].bitcast(BF16)

    def rstd_of(ss, sq, rs, n, add_ss=None):
        act(sq, ss, AF.Ln, bias=EPS, scale=1.0 / n, sr=[ss])
        act(rs, sq, AF.Exp, scale=-0.5, sr=[sq])

    es1 = ExitStack()
    cqnT = sb(es1, "cqnT", [128, 3, 2048], BF16)
    ckvnT = sb(es1, "ckvnT", [128, 2, 4096], BF16)
    kropeR = sb(es1, "kropeR", [128, 32, 32])
    krss_all = sb(es1, "krss_all", [128, 32])
    kgcol = sb(es1, "kgcol", [96, 1])
    qgcol = sb(es1, "qgcol", [96, 1])
    memset("pool", kgcol[:], 1.0)
    memset("pool", qgcol[:], 1.0)
    dma("sync", kgcol[0:64, 0:1], kng_d[0:1, 0:64].rearrange("o n -> n o"), "su_kg")
    dma("sync", qgcol[0:64, 0:1], qng_d[0:1, 0:64].rearrange("o n -> n o"), "su_qg")
    wuq = wload(es1, "wuq", w_uq, 3, 0, 768, "w_uq")
    wukv = wload(es1, "wukv", w_ukv, 2, 0, 1024, "w_ukv")

    esA = ExitStack()
    win_kv = wload(esA, "win_kv", w_in, 8, C_CKV, 288, "w_in_kv")
    win_f = wload(esA, "win_f", w_in, 8, C_HF, 512, "w_in_f")
    win_i = wload(esA, "win_i", w_in, 8, C_HI, 512, "w_in_i")
    win_q = wload(esA, "win_q", w_in, 8, C_CQ, 384, "w_in_q")
    win_hq = wload(esA, "win_hq", w_in, 8, C_HQ, 512, "w_in_hq")
    win_g = wload(esA, "win_g", w_in, 8, C_HG, 512, "w_in_g")
    xt = [sb(esA, f"xt{i}", [128, 1024]) for i in range(2)]
    junk = sb(esA, "junk", [128, 1024], BF16)
    krope = sb(esA, "krope", [128, 32, 32])
    krg2 = [sb(esA, f"krg{i}", [128, 32]) for i in range(2)]
    kra2 = [sb(esA, f"kra{i}", [128, 16]) for i in range(2)]
    krb2 = [sb(esA, f"krb{i}", [128, 16]) for i in range(2)]
    junkr = sb(esA, "junkr", [128, 32])
    hn = sb(esA, "hn", [128, 1024], BF16)
    hT = [sb(esA, f"hT{i}", [128, 8, 128], BF16) for i in range(2)]
    st_small = [dict(ss=sb(esA, f"ss{i}", [128, 4]), sq=sb(esA, f"sq{i}", [128, 4]), rs=sb(esA, f"rs{i}", [128, 4]))
                for i in range(2)]
    lat_s = [dict(ss=sb(esA, f"lss{i}", [128, 2]), sq=sb(esA, f"lsq{i}", [128, 2]), rs=sb(esA, f"lrs{i}", [128, 2]))
             for i in range(2)]
    latn = sb(esA, "latn", [128, 640], BF16)
    sgn = sb(esA, "sgn", [128, 512])
    kk = sb(esA, "kk", [128, 512])
    gl = sb(esA, "gl", [128, 512])
    etmp = sb(esA, "etmp", [128, 512])
    etmp2 = sb(esA, "etmp2", [128, 512])
    kt_2 = [sb(esA, f"kt{i}", [128, 512], BF16) for i in range(2)]
    qt_2 = [sb(esA, f"qt{i}", [128, 512], BF16) for i in range(2)]
    vt_2 = [sb(esA, f"vt{i}", [128, 512], BF16) for i in range(2)]
    sl_ = sb(esA, "sl", [128, 512])
    slg2 = [sb(esA, f"slg{i}", [128, 512]) for i in range(2)]
    kqT2 = [sb(esA, f"kqT{i}", [128, 8, 128], BF16) for i in range(2)]
    ATm2 = [sb(esA, f"ATm{i}", [128, 4, 128], BF16) for i in range(2)]
    esc = sb(esA, "esc", [128, 4, 4])
    eL = sb(esA, "eL", [128, 4, 2])
    Sst = sb(esA, "Sst", [128, 4, 128])
    Spb = [sb(esA, f"Spb{i}", [128, 4, 128], BF16) for i in range(2)]
    Tst = sb(esA, "Tst", [128, 4, 128])
    Ust = sb(esA, "Ust", [128, 4, 128])
    osq = sb(esA, "osq", [128, 512])
    oss = sb(esA, "oss", [128, 4])
    osr = sb(esA, "osr", [128, 4])
    ors = sb(esA, "ors", [128, 4])
    o1 = sb(esA, "o1", [128, 512])
    rec = sb(esA, "rec", [128, 512], BF16)

    memset("dve", Sst[:], 0.0)

    def norm_tile_to_hT(x_ap, sm, hT_t, gain, junk_t, hn_t):
        act(junk_t[:], x_ap, AF.Square, accum=sm["ss"][:, 0:1])
        if substage < 0.2:
            return
        rstd_of(sm["ss"][:, 0:1], sm["sq"][:, 0:1], sm["rs"][:, 0:1], 1024)
        if substage < 0.3:
            return
        act(hn_t[:], x_ap, AF.Identity, scale=sm["rs"][:, 0:1])
        if substage < 0.4:
            return
        bk = nb()
        bv = bf(bk).rearrange("p (c t) -> p c t", c=8)
        for c in range(8):
            tr(bv[:, c, :], hn_t[:, c * 128:(c + 1) * 128], ident[:])
        if substage < 0.5:
            return
        tt("dve", hT_t[:], bv, gain[:, :].unsqueeze(2).to_broadcast([128, 8, 128]), ALU.mult)

    def proj(hT_t, w, col0, ncols):
        bk = nb()
        for c in range(8):
            mm(bk[:, 0:ncols], hT_t[:, c, :], w[:, c, col0:col0 + ncols], c == 0, c == 7)
        return bk

    for t in (range(NT) if a_tiles is None else a_tiles):
        own = t >= OWN0
        o = t - OWN0
        s = t % 2
        dma("sync", xt[s][:], xw[t * 128:(t + 1) * 128, :], f"x{s}")
        norm_tile_to_hT(xt[s][:], st_small[s], hT[s], mixg, junk, hn)
        if substage < 1:
            continue
        ls = lat_s[s]
        kt_, qt_, vt_, slg, kqT, ATm = kt_2[s], qt_2[s], vt_2[s], slg2[s], kqT2[s], ATm2[s]
        b_kv = proj(hT[s], win_kv, 0, 288)
        act(junk[:, 0:256], b_kv[:, 0:256], AF.Square, accum=ls["ss"][:, 0:1])
        cp("dve", krope[:, t, :], b_kv[:, 256:288])
        act(junkr[:], krope[:, t, :], AF.Square, accum=krss_all[:, t:t + 1])
        krg_, kra_, krb_ = krg2[s], kra2[s], krb2[s]
        tt("pool", krg_[:], krope[:, t, :], kng[:, 64:96], ALU.mult)
        sin_t, cos_t = cs[:, t, 0:16], cs[:, t, 16:32]
        tt("pool", kra_[:], krg_[:, 0:16], cos_t, ALU.mult)
        tt("pool", krb_[:], krg_[:, 16:32], sin_t, ALU.mult)
        tt("pool", kropeR[:, t, 0:16], kra_[:], krb_[:], ALU.subtract)
        tt("pool", kra_[:], krg_[:, 16:32], cos_t, ALU.mult)
        tt("pool", krb_[:], krg_[:, 0:16], sin_t, ALU.mult)
        tt("pool", kropeR[:, t, 16:32], kra_[:], krb_[:], ALU.add)
        if substage < 1.1:
            continue
        if own:
            b_q = proj(hT[s], win_q, 0, 384)
            act(junk[:, 256:640], b_q[:, 0:384], AF.Square, accum=ls["ss"][:, 1:2])
        act(ls["sq"][:, 0:1], ls["ss"][:, 0:1], AF.Ln, bias=EPS, scale=1.0 / 256, sr=[ls["ss"]])
        if own:
            act(ls["sq"][:, 1:2], ls["ss"][:, 1:2], AF.Ln, bias=EPS, scale=1.0 / 384, sr=[ls["ss"]])
        nls = 2 if own else 1
        act(ls["rs"][:, 0:nls], ls["sq"][:, 0:nls], AF.Exp, scale=-0.5, sr=[ls["sq"]])
        if substage < 1.2:
            continue
        act(latn[:, 0:256], b_kv[:, 0:256], AF.Identity, scale=ls["rs"][:, 0:1])
        if own:
            act(latn[:, 256:640], b_q[:, 0:384], AF.Identity, scale=ls["rs"][:, 1:2])
        bk = nb()
        bv = bf(bk).rearrange("p (c t) -> p c t", c=8)
        if substage < 1.3:
            continue
        for c in range(5 if own else 2):
            tr(bv[:, c, :], latn[:, c * 128:(c + 1) * 128], ident[:])
        if substage < 1.4:
            continue
        tt("dve", ckvnT[:, :, t * 128:(t + 1) * 128], bv[:, 0:2, :],
           kvag[:, :].unsqueeze(2).to_broadcast([128, 2, 128]), ALU.mult)
        if own:
            tt("dve", cqnT[:, :, o * 128:(o + 1) * 128], bv[:, 2:5, :],
               qag[:, :].unsqueeze(2).to_broadcast([128, 3, 128]), ALU.mult)
        if substage < 2:
            continue
        b_f = proj(hT[s], win_f, 0, 512)
        if own:
            b_g = proj(hT[s], win_g, 0, 512)
        act(sgn[:], b_f[:], AF.Sigmoid, scale=-1.0)
        if own:
            act(sl_[:], b_g[:], AF.Sigmoid)
            tt("dve", sl_[:], b_g[:], sl_[:], ALU.mult)
            tt("pool", slg[:].rearrange("p (h v) -> p h v", h=4), sl_[:].rearrange("p (h v) -> p h v", h=4),
               hgg[:, :].unsqueeze(1).to_broadcast([128, 4, 128]), ALU.mult)
        tt("dve", kk[:], sgn[:], omlb[:], ALU.mult)
        act(gl[:], kk[:], AF.Ln, bias=1.0, scale=-1.0)
        b_D = nb()
        mm(b_D[:], Dmat[:], gl[:], True, True)
        b_s = nb()
        for h in range(4):
            mm(b_s[:, h * 4:(h + 1) * 4], gl[:, h * 128:(h + 1) * 128], Sel[:], True, True)
        b_i = proj(hT[s], win_i, 0, 512)
        cp("dve", vt_[:], b_i[:])
        act(etmp[:], b_D[:], AF.Exp, scale=-1.0)
        tt("dve", kt_[:], kk[:], etmp[:], ALU.mult)
        act(esc[:].rearrange("p h f -> p (h f)"), b_s[:, 0:16], AF.Exp)
        tt("dve", eL[:], esc[:].rearrange("p h (c two) -> p h c two", two=2)[:, :, :, 0],
           esc[:].rearrange("p h (c two) -> p h c two", two=2)[:, :, :, 1], ALU.mult)
        if own:
            b_hq = proj(hT[s], win_hq, 0, 512)
            act(etmp2[:], b_D[:], AF.Exp)
            tt("dve", qt_[:], b_hq[:], etmp2[:], ALU.mult)
            bk2 = nb()
            bv2 = bf(bk2).rearrange("p (c t) -> p c t", c=8)
            for h in range(4):
                tr(bv2[:, h, :], kt_[:, h * 128:(h + 1) * 128], ident[:])
            for h in range(4):
                tr(bv2[:, 4 + h, :], qt_[:, h * 128:(h + 1) * 128], ident[:])
            cp("dve", kqT[:], bv2)
            b_A = nb()
            for h in range(4):
                mm(b_A[:, h * 128:(h + 1) * 128], kqT[:, h, :], kqT[:, 4 + h, :], True, True)
            tt("dve", ATm[:], b_A[:].rearrange("p (h t) -> p h t", h=4),
               mblk[:, :].unsqueeze(1).to_broadcast([128, 4, 128]), ALU.mult)
            b_O = nb()
            for h in range(4):
                mm(b_O[:, h * 128:(h + 1) * 128], ATm[:, h, :], vt_[:, h * 128:(h + 1) * 128], h == 0, False)
        if substage < 3:
            continue
        b_KV = [nb(), nb()]
        for c in range(2):
            for h in range(4):
                mm(b_KV[c][:, h * 128:(h + 1) * 128], kt_[c * 64:(c + 1) * 64, h * 128:(h + 1) * 128],
                   vt_[c * 64:(c + 1) * 64, h * 128:(h + 1) * 128], True, True)
        for c in range(2):
            eA = esc[:, :, 2 * c:2 * c + 1].to_broadcast([128, 4, 128])
            eB = esc[:, :, 2 * c + 1:2 * c + 2].to_broadcast([128, 4, 128])
            eLc = eL[:, :, c:c + 1].to_broadcast([128, 4, 128])
            if own:
                tt("dve", Spb[c][:], Sst[:], eB, ALU.mult)
                for h in range(4):
                    mm(b_O[c * 64:(c + 1) * 64, h * 128:(h + 1) * 128], kqT[:, 4 + h, c * 64:(c + 1) * 64],
                       Spb[c][:, h, :], False, (c == 1 and h == 3))
            tt("pool", Tst[:], Sst[:], eLc, ALU.mult)
            tt("dve", Ust[:], b_KV[c][:].rearrange("p (h v) -> p h v", h=4), eA, ALU.mult)
            tt("dve", Sst[:], Tst[:], Ust[:], ALU.add)
        if substage < 4:
            continue
        if own:
            act(osq[:], b_O[:], AF.Square)
            rsum(oss[:], osq[:].rearrange("p (h v) -> p h v", h=4))
            rstd_of(oss[:], osr[:], ors[:], 128)
            tt("dve", o1[:].rearrange("p (h v) -> p h v", h=4), b_O[:].rearrange("p (h v) -> p h v", h=4),
               ors[:, :].unsqueeze(2).to_broadcast([128, 4, 128]), ALU.mult)
            tt("dve", rec[:], o1[:], slg[:], ALU.mult)
            bk3 = nb()
            bv3 = bf(bk3).rearrange("p (c t) -> p c t", c=8)
            for h in range(4):
                tr(bv3[:, h, :], rec[:, h * 128:(h + 1) * 128], ident[:])
            cp("dve", recT[:, :, o * 128:(o + 1) * 128], bv3[:, 0:4, :])
    dump("recT", recT[:].rearrange("p a b -> p (a b)"))
    dump("cqnT", cqnT[:].rearrange("p a b -> p (a b)"))
    dump("ckvnT", ckvnT[:].rearrange("p a b -> p (a b)"))
    dump("Sst", Sst[:].rearrange("p a b -> p (a b)"))
    if stage == "A":
        P.muted = True
    P.barrier()
    esA.close()

    esB = ExitStack()
    KT = sb(esB, "KT", [96, 4, 4096], BF16)
    Vt = sb(esB, "Vt", [128, 32, 4, 65], BF16)
    NKB = 3
    kvsq2 = [sb(esB, f"kvsq{i}", [128, 512]) for i in range(NKB)]
    kss2 = [sb(esB, f"kss{i}", [128, 4]) for i in range(NKB)]
    ksq2 = [sb(esB, f"ksq{i}", [128, 4]) for i in range(NKB)]
    krs2 = [sb(esB, f"krs{i}", [128, 4]) for i in range(NKB)]
    kfull2 = [sb(esB, f"kfull{i}", [128, 4, 96], BF16) for i in range(NKB)]
    kr42 = [sb(esB, f"kr4{i}", [128, 4, 32]) for i in range(NKB)]
    ra2 = [sb(esB, f"ra{i}", [128, 4, 16]) for i in range(NKB)]
    rb2 = [sb(esB, f"rb{i}", [128, 4, 16]) for i in range(NKB)]
    QT = [sb(esB, f"QT{i}", [96, 4, 512], BF16) for i in range(2)]
    PT = [sb(esB, f"PT{i}", [128, 512], BF16) for i in range(4)]
    dn = sb(esB, "dn", [128, 512])
    rden = sb(esB, "rden", [64, 512])
    SCALE = float(96 ** -0.5)

    def rope(eng, dst, src, t, nh, ra, rb):
        sin_b = cs[:, t, 0:16].unsqueeze(1).to_broadcast([128, nh, 16])
        cos_b = cs[:, t, 16:32].unsqueeze(1).to_broadcast([128, nh, 16])
        x1, x2 = src[:, :, 0:16], src[:, :, 16:32]
        tt(eng, ra[:, 0:nh, :], x1, cos_b, ALU.mult)
        tt(eng, rb[:, 0:nh, :], x2, sin_b, ALU.mult)
        tt(eng, dst[:, :, 64:80], ra[:, 0:nh, :], rb[:, 0:nh, :], ALU.subtract)
        tt(eng, ra[:, 0:nh, :], x2, cos_b, ALU.mult)
        tt(eng, rb[:, 0:nh, :], x1, sin_b, ALU.mult)
        tt(eng, dst[:, :, 80:96], ra[:, 0:nh, :], rb[:, 0:nh, :], ALU.add)

    for half in range(2):
        for t in range(NT):
            u = t % 3
            bk = nb()
            for c in range(2):
                mm(bk[:], ckvnT[:, c, t * 128:(t + 1) * 128], wukv[:, c, half * 512:(half + 1) * 512], c == 0, c == 1)
            kv3 = bk[:].rearrange("p (h d) -> p h d", h=4)
            sq3 = kvsq2[u][:].rearrange("p (h d) -> p h d", h=4)
            act(sq3[:, :, 0:64], kv3[:, :, 0:64], AF.Square)
            rsum(kss2[u][:, 0:4], sq3[:, :, 0:64])
            ts("dve", kss2[u][:, 0:4], kss2[u][:, 0:4], krss_all[:, t:t + 1], None, ALU.add)
            rstd_of(kss2[u][:, 0:4], ksq2[u][:, 0:4], krs2[u][:, 0:4], 96)
            kf = kfull2[u]
            tt("dve", kf[:, :, 0:64], kv3[:, :, 0:64], krs2[u][:, 0:4].unsqueeze(2).to_broadcast([128, 4, 64]), ALU.mult)
            tt("dve", kf[:, :, 64:96], kropeR[:, t, :].unsqueeze(1).to_broadcast([128, 4, 32]),
               krs2[u][:, 0:4].unsqueeze(2).to_broadcast([128, 4, 32]), ALU.mult)
            bk2 = nb()
            bv2 = bf(bk2).rearrange("p (c t) -> p c t", c=8)
            for h in range(4):
                tr(bv2[0:96, h, :], kf[:, h, :], ident[:])
            act(KT[:, :, t * 128:(t + 1) * 128], bv2[0:96, 0:4, :], AF.Identity, scale=kgcol[:, 0:1])
            act(Vt[:, t, :, 0:64], kv3[:, :, 64:128], AF.Identity, scale=kval[:, t:t + 1])
            cp("pool", Vt[:, t, :, 64:65], kval[:, t:t + 1].unsqueeze(1).to_broadcast([128, 4, 1]))
        for g in range(4):
            QTg = QT[g % 2]
            for i in range(4):
                o = g * 4 + i
                t = OWN0 + o
                u = o % 3
                bk = nb((3, 4))
                for c in range(3):
                    mm(bk[:, 0:384], cqnT[:, c, o * 128:(o + 1) * 128], wuq[:, c, half * 384:(half + 1) * 384],
                       c == 0, c == 2)
                q3 = bk[:, 0:384].rearrange("p (h d) -> p h d", h=4)
                act(kvsq2[u][:, 0:384], bk[:, 0:384], AF.Square)
                rsum(kss2[u][:, 0:4], kvsq2[u][:, 0:384].rearrange("p (h d) -> p h d", h=4))
                rstd_of(kss2[u][:, 0:4], ksq2[u][:, 0:4], krs2[u][:, 0:4], 96)
                kf = kfull2[u]
                tt("dve", kf[:, :, 0:64], q3[:, :, 0:64], krs2[u][:, 0:4].unsqueeze(2).to_broadcast([128, 4, 64]), ALU.mult)
                tt("dve", kr42[u][:], q3[:, :, 64:96], krs2[u][:, 0:4].unsqueeze(2).to_broadcast([128, 4, 32]), ALU.mult)
                tt("pool", kr42[u][:], kr42[u][:], qng[:, 64:96].unsqueeze(1).to_broadcast([128, 4, 32]), ALU.mult)
                rope("pool", kf, kr42[u], t, 4, ra2[u], rb2[u])
                bk2 = nb((3, 4))
                bv2 = bf(bk2).rearrange("p (c t) -> p c t", c=8)
                for h in range(4):
                    tr(bv2[0:96, h, :], kf[:, h, :], ident[:])
                act(QTg[:, :, i * 128:(i + 1) * 128], bv2[0:96, 0:4, :], AF.Identity, scale=qgcol[:, 0:1])
            nkt = OWN0 + 4 * g + 4
            for h in range(4):
                hh = half * 4 + h
                b_o = banks[6 + (h % 2)]
                pend = []
                for it in range(nkt + 2):
                    if it < nkt:
                        kt = it
                        j = kt - (OWN0 + 4 * g)
                        qlo = j * 128 if j > 0 else 0
                        b_sc = banks[it % 3]
                        mm(b_sc[:, qlo:512], KT[:, h, kt * 128:(kt + 1) * 128], QTg[:, h, qlo:512], True, True)
                        pt = PT[it % 4]
                        act(pt[:, qlo:512], b_sc[:, qlo:512], AF.Exp, scale=SCALE)
                        if j >= 0:
                            tt("pool", pt[:, qlo:qlo + 128], pt[:, qlo:qlo + 128], maskT[:], ALU.mult)
                        pend.append((kt, qlo, pt))
                    if it >= 2:
                        kt, qlo, pt = pend[it - 2]
                        mm(b_o[0:65, qlo:512], Vt[:, kt, h, :], pt[:, qlo:512], kt == 0, kt == nkt - 1)
                cp("dve", dn[64:65, :], b_o[64:65, :])
                b_d = banks[5]
                mm(b_d[0:64, :], ones_f[64:65, 0:64], dn[64:65, :], True, True)
                recip(rden[:], b_d[0:64, :])
                po = (hh % 2) * 64
                tt("dve", attnT[po:po + 64, hh // 2, g * 512:(g + 1) * 512], b_o[0:64, :], rden[:], ALU.mult)
    dump("attnT", attnT[:].rearrange("p a b -> p (a b)"))
    dump("KT", KT[:].rearrange("p a b -> p (a b)"))
    if stage == "B":
        P.muted = True
    P.barrier()
    esB.close()
    es1.close()
    esS.close()

    esC = ExitStack()
    wg = wload(esC, "wg", w_in, 8, C_GT, 2048, "wg", split=4)
    wb = wload(esC, "wb", w_branch, 8, 0, 1024, "wb", split=2)
    xc = [sb(esC, f"xc{i}", [128, 1024]) for i in range(2)]
    junkc = sb(esC, "junkc", [128, 1024], BF16)
    hnc = sb(esC, "hnc", [128, 1024], BF16)
    hTc = [sb(esC, f"hTc{i}", [128, 8, 128], BF16) for i in range(2)]
    smc = [dict(ss=sb(esC, f"css{i}", [128, 4]), sq=sb(esC, f"csq{i}", [128, 4]), rs=sb(esC, f"crs{i}", [128, 4]))
           for i in range(2)]
    gts = sb(esC, "gts", [128, 2048])
    m1 = sb(esC, "m1", [128, 512])
    m2 = sb(esC, "m2", [128, 512])
    mtok = sb(esC, "mtok", [128, 1024], BF16)

    for o in range(NOWN):
        t = OWN0 + o
        s = o % 2
        dma("sync", xc[s][:], xw[t * 128:(t + 1) * 128, :], f"xc{s}")
        norm_tile_to_hT(xc[s][:], smc[s], hTc[s], mixg, junkc, hnc)
        for gq in range(4):
            bk = proj(hTc[s], wg, gq * 512, 512)
            act(gts[:, gq * 512:(gq + 1) * 512], bk[:], AF.Sigmoid)
        for hf in range(2):
            b_a = nb()
            for j in range(4):
                mm(b_a[:], attnT[:, j, o * 128:(o + 1) * 128], wb[:, j, hf * 512:(hf + 1) * 512], j == 0, j == 3)
            b_r = nb()
            for j in range(4):
                mm(b_r[:], recT[:, j, o * 128:(o + 1) * 128], wb[:, 4 + j, hf * 512:(hf + 1) * 512], j == 0, j == 3)
            tt("dve", m1[:], b_a[:], gts[:, hf * 512:(hf + 1) * 512], ALU.mult)
            tt("dve", m2[:], b_r[:], gts[:, 1024 + hf * 512:1024 + (hf + 1) * 512], ALU.mult)
            tt("pool", mtok[:, hf * 512:(hf + 1) * 512], m1[:], m2[:], ALU.add)
        bk = nb()
        bv = bf(bk).rearrange("p (c t) -> p c t", c=8)
        for c in range(8):
            tr(bv[:, c, :], mtok[:, c * 128:(c + 1) * 128], ident[:])
        cp("dve", attnT[:, :, o * 128:(o + 1) * 128], bv[:, 0:4, :])
        cp("dve", recT[:, :, o * 128:(o + 1) * 128], bv[:, 4:8, :])
    P.barrier()
    esC.close()

    es2 = ExitStack()
    x1 = sb(es2, "x1", [128, 16, 1024])
    h2T = sb(es2, "h2T", [128, 8, 2048], BF16)
    esC2 = ExitStack()
    wo = wload(esC2, "wo", w_out, 8, 0, 1024, "wo", split=2)
    xd = [sb(esC2, f"xd{i}", [128, 1024]) for i in range(2)]
    junkd = sb(esC2, "junkd", [128, 1024], BF16)
    hnd = sb(esC2, "hnd", [128, 1024], BF16)
    smc2 = [dict(ss=sb(esC2, f"dss{i}", [128, 4]), sq=sb(esC2, f"dsq{i}", [128, 4]), rs=sb(esC2, f"drs{i}", [128, 4]))
            for i in range(2)]
    h2s = sb(esC2, "h2s", [128, 8, 128], BF16)
    for o in range(NOWN):
        t = OWN0 + o
        s = o % 2
        dma("sync", xd[s][:], xw[t * 128:(t + 1) * 128, :], f"xd{s}")
        for hf in range(2):
            b_z = nb()
            for c in range(8):
                src = attnT if c < 4 else recT
                mm(b_z[:], src[:, c % 4, o * 128:(o + 1) * 128], wo[:, c, hf * 512:(hf + 1) * 512], c == 0, c == 7)
            tt("dve", x1[:, o, hf * 512:(hf + 1) * 512], b_z[:], xd[s][:, hf * 512:(hf + 1) * 512], ALU.add)
        norm_tile_to_hT(x1[:, o, :], smc2[s], h2s, ffng, junkd, hnd)
        cp("pool", h2T[:, :, o * 128:(o + 1) * 128], h2s[:])
    dump("x1", x1[:].rearrange("p a b -> p (a b)"))
    if stage == "C":
        P.muted = True
    P.barrier()
    esC2.close()

    esD = ExitStack()
    NBUF = 3
    hidT2 = [sb(esD, f"hidT{i}", [128, 2, 512], BF16) for i in range(2)]
    sg = [sb(esD, f"sg{i}", [128, 512]) for i in range(2)]
    wfgb = [sb(esD, f"wfgb{i}", [128, 8, 256], BF16) for i in range(NBUF)]
    wfub = [sb(esD, f"wfub{i}", [128, 8, 256], BF16) for i in range(NBUF)]
    wfdb = [sb(esD, f"wfdb{i}", [128, 2, 1024], BF16) for i in range(NBUF)]
    vfg = w_fg.rearrange("(c p) n -> p c n", p=128)
    vfu = w_fu.rearrange("(c p) n -> p c n", p=128)
    vfd = w_fd.rearrange("(c p) n -> p c n", p=128)
    NGRP = 11

    def ffn_load(gi):
        i = gi % NBUF
        for c in range(8):
            dma("pool", wfgb[i][:, c, :], vfg[:, c, gi * 256:(gi + 1) * 256], f"fg{i}", nowaw=True, max_dma_last_dim=4096)
        for c in range(8):
            dma("pool", wfub[i][:, c, :], vfu[:, c, gi * 256:(gi + 1) * 256], f"fu{i}", nowaw=True, max_dma_last_dim=4096)
        for c0 in range(2):
            dma("pool", wfdb[i][:, c0, :], vfd[:, gi * 2 + c0, :], f"fd{i}", nowaw=True, max_dma_last_dim=4096)

    ffn_load(0)
    ffn_load(1)
    cnt = 0
    for gi in range(NGRP):
        if gi + 2 < NGRP:
            ffn_load(gi + 2)
        i = gi % NBUF
        for tb in range(4):
            hid = hidT2[cnt % 2]
            cnt += 1
            for hc in range(2):
                b_g = nb()
                for c in range(8):
                    mm(b_g[:], wfgb[i][:, c, hc * 128:(hc + 1) * 128], h2T[:, c, tb * 512:(tb + 1) * 512], c == 0, c == 7)
                b_u = nb()
                for c in range(8):
                    mm(b_u[:], wfub[i][:, c, hc * 128:(hc + 1) * 128], h2T[:, c, tb * 512:(tb + 1) * 512], c == 0, c == 7)
                sgi = sg[hc % 2]
                act(sgi[:], b_g[:], AF.Silu)
                tt("dve", hid[:, hc, :], b_u[:], sgi[:], ALU.mult)
            for i4 in range(4):
                o = tb * 4 + i4
                for hf in range(2):
                    b_d = nb()
                    for hc in range(2):
                        mm(b_d[:], hid[:, hc, i4 * 128:(i4 + 1) * 128], wfdb[i][:, hc, hf * 512:(hf + 1) * 512],
                           hc == 0, hc == 1)
                    tt("dve", x1[:, o, hf * 512:(hf + 1) * 512], b_d[:], x1[:, o, hf * 512:(hf + 1) * 512], ALU.add)
    dump("x2", x1[:].rearrange("p a b -> p (a b)"))
    if stage == "D":
        P.muted = True
    P.barrier()
    esD.close()

    esE = ExitStack()
    wpg = wload(esE, "wpg", w_pg, 8, 0, 1024, "wpg", split=2)
    wpp = wload(esE, "wpp", w_pp, 2, 0, 1024, "wpp")
    pt_ = [sb(esE, f"pt{i}", [128, 256]) for i in range(2)]
    pb_2 = [sb(esE, f"pb{i}", [128, 256], BF16) for i in range(2)]
    pT2 = [sb(esE, f"pT{i}", [128, 2, 128], BF16) for i in range(2)]
    junke = sb(esE, "junke", [128, 1024], BF16)
    hne2 = [sb(esE, f"hne{i}", [128, 1024], BF16) for i in range(2)]
    h3T2 = [sb(esE, f"h3T{i}", [128, 8, 128], BF16) for i in range(2)]
    sme = [dict(ss=sb(esE, f"ess{i}", [128, 4]), sq=sb(esE, f"esq{i}", [128, 4]), rs=sb(esE, f"ers{i}", [128, 4]))
           for i in range(2)]
    sme2 = [dict(ss=sb(esE, f"fss{i}", [128, 4]), sq=sb(esE, f"fsq{i}", [128, 4]), rs=sb(esE, f"frs{i}", [128, 4]))
            for i in range(2)]
    sgm2 = [sb(esE, f"sgm{i}", [128, 1024]) for i in range(2)]
    e12 = [sb(esE, f"e1{i}", [128, 1024]) for i in range(2)]
    yo = [sb(esE, f"yo{i}", [128, 1024]) for i in range(2)]
    for o in range(NOWN):
        s = o % 2
        pb_, pT, hne, h3T, sgm, e1 = pb_2[s], pT2[s], hne2[s], h3T2[s], sgm2[s], e12[s]
        dma("sync", pt_[s][:], pw[o * 128:(o + 1) * 128, :], f"p{s}")
        cp("pool", pb_[:], pt_[s][:])
        bk = nb()
        bv = bf(bk).rearrange("p (c t) -> p c t", c=8)
        for c in range(2):
            tr(bv[:, c, :], pb_[:, c * 128:(c + 1) * 128], ident[:])
        cp("dve", pT[:], bv[:, 0:2, :])
        b_e = [nb(), nb()]
        sm = sme[s]
        for hf in range(2):
            for c in range(2):
                mm(b_e[hf][:], pT[:, c, :], wpp[:, c, hf * 512:(hf + 1) * 512], c == 0, c == 1)
            act(junke[:, hf * 512:(hf + 1) * 512], b_e[hf][:], AF.Square, accum=sm["ss"][:, hf:hf + 1])
        ts("dve", sm["ss"][:, 2:3], sm["ss"][:, 0:1], sm["ss"][:, 1:2], None, ALU.add)
        rstd_of(sm["ss"][:, 2:3], sm["sq"][:, 2:3], sm["rs"][:, 2:3], 1024)
        norm_tile_to_hT(x1[:, o, :], sme2[s], h3T, plgg, junke, hne)
        for hf in range(2):
            b_g = nb()
            for c in range(8):
                mm(b_g[:], h3T[:, c, :], wpg[:, c, hf * 512:(hf + 1) * 512], c == 0, c == 7)
            act(sgm[:, hf * 512:(hf + 1) * 512], b_g[:], AF.Sigmoid)
            stt(e1[:, hf * 512:(hf + 1) * 512], b_e[hf][:], sm["rs"][:, 2:3], ppg[:, hf * 512:(hf + 1) * 512],
                ALU.mult, ALU.mult)
        tt("pool", e1[:], e1[:], sgm[:], ALU.mult)
        tt("dve", yo[s][:], e1[:], x1[:, o, :], ALU.add)
        dma("sync", y[o * 128:(o + 1) * 128, :], yo[s][:], f"y{s}", writes=[("yout", s)])
    P.muted = False
    P.add("sync", None, reads=[("yout", 0), ("yout", 1)] + [d for d in dbg_out.values()])

    P.finalize()
    chan_names = sorted(P.chan_n.keys())
    with ExitStack() as ess:
        sems = {e: ess.enter_context(nc.semaphore(f"s_{e}")) for e in ("act", "dve", "pool", "pe")}
        chans = {c: ess.enter_context(nc.semaphore(f"c_{c}")) for c in chan_names}
        block = ess.enter_context(nc.Block())

        @block.sync
        def _(e):
            P.emit("sync", e, sems, chans)

        @block.scalar
        def _(e):
            P.emit("act", e, sems, chans)

        @block.vector
        def _(e):
            P.emit("dve", e, sems, chans)

        @block.gpsimd
        def _(e):
            P.emit("pool", e, sems, chans)

        @block.tensor
        def _(e):
            P.emit("pe", e, sems, chans)
    esE.close()
    es2.close()
    es0.close()
    return nc


_NC_CACHE = {}


def _pc(v, k):
    return np.ascontiguousarray(np.asarray(v, np.float32).reshape(k, 128).T)


def kernel(x, p, positions, mix_norm_g, w_in, q_a_norm_g, w_uq, kv_a_norm_g, w_ukv, q_norm_g, k_norm_g,
           hg_lb_logits, hg_out_norm_g, w_branch, w_out, ffn_norm_g, w_ffn_gate, w_ffn_up, w_ffn_down,
           ple_gate_norm_g, w_ple_gate, w_ple_proj, ple_post_norm_g):
    x = np.asarray(x, np.float32)
    p = np.asarray(p, np.float32)
    positions = np.asarray(positions, np.int32)
    f = lambda a: np.ascontiguousarray(np.asarray(a, np.float32))
    shared = {
        "w_in": f(w_in[0]), "w_uq": f(w_uq[0]), "w_ukv": f(w_ukv[0]),
        "w_branch": f(np.asarray(w_branch)[0].reshape(1024, 1024)), "w_out": f(w_out[0]),
        "w_fg": f(w_ffn_gate[0]), "w_fu": f(w_ffn_up[0]), "w_fd": f(w_ffn_down[0]),
        "w_pg": f(w_ple_gate[0]), "w_pp": f(w_ple_proj[0]),
        "mixg": _pc(mix_norm_g[0], 8), "ffng": _pc(ffn_norm_g[0], 8), "plgg": _pc(ple_gate_norm_g[0], 8),
        "qag": _pc(q_a_norm_g[0], 3), "kvag": _pc(kv_a_norm_g[0], 2),
        "qng": f(q_norm_g[0]).reshape(1, 96), "kng": f(k_norm_g[0]).reshape(1, 96),
        "hgg": f(hg_out_norm_g[0]).reshape(1, 128), "ppg": f(ple_post_norm_g[0]).reshape(1, 1024),
        "lbl": f(hg_lb_logits),
    }
    in_maps = []
    for core in range(8):
        b, j = core // 2, core % 2
        if j == 1:
            xwin = x[b]
            pos = positions[b]
            kv = np.ones(4096, np.float32)
        else:
            xwin = np.concatenate([np.zeros((2048, 1024), np.float32), x[b, :2048]], axis=0)
            pos = np.concatenate([np.zeros(2048, np.int32), positions[b, :2048]])
            kv = np.concatenate([np.zeros(2048, np.float32), np.ones(2048, np.float32)])
        m = dict(shared)
        m["xw"] = np.ascontiguousarray(xwin)
        m["pw"] = np.ascontiguousarray(p[0, b, j * 2048:(j + 1) * 2048])
        m["posw"] = np.ascontiguousarray(pos.reshape(32, 128).T.astype(np.int32))
        m["kvalw"] = np.ascontiguousarray(kv.reshape(32, 128).T)
        in_maps.append(m)
    if _NC_CACHE.get("maps_only"):
        return in_maps
    if "nc" not in _NC_CACHE:
        _NC_CACHE["nc"] = build_nc()
    res = run_bass_kernel_spmd(_NC_CACHE["nc"], in_maps, core_ids=list(range(8)))
    out = np.empty((4, 4096, 1024), np.float32)
    for core in range(8):
        b, j = core // 2, core % 2
        out[b, j * 2048:(j + 1) * 2048] = res.results[core]["y"]
    return out
```
